# Optimizing a Trainium2 kernel written in Bass

```python
import jax, jax.numpy as jnp
from jax import lax
import numpy as np

D_MODEL = 2048
BATCH = 4
SEQ = 4096
DEPTH = 1

PLE_DIM = 256
D_FF = 4 * D_MODEL
EPS = 1e-6
GLA_HEADS = 4
GLA_KEY_W = D_MODEL // 4
GLA_VAL_W = D_MODEL // 2
GLA_DK = GLA_KEY_W // GLA_HEADS
GLA_DV = GLA_VAL_W // GLA_HEADS
GATE_RANK = 16
GATE_TAU = 16.0
CHUNK = 64
SB_W = D_MODEL // 2
SB_DH = 128
SB_HEADS = SB_W // SB_DH
SB_BLOCK = 128
IN_SPLITS = (GLA_KEY_W, GLA_KEY_W, GLA_VAL_W, GATE_RANK, GLA_VAL_W,
             SB_W, SB_W, SB_W, D_MODEL, D_MODEL)
IN_W = sum(IN_SPLITS)

kernel_name = "hybrid_gla_stickbreaking_gated_block"


def rmsnorm(h, gain):
    hf = h.astype(jnp.float32)
    hf = hf * lax.rsqrt(jnp.mean(hf * hf, axis=-1, keepdims=True) + EPS)
    return (hf * gain.astype(jnp.float32)).astype(h.dtype)


def split_cols(z, sizes):
    offs = np.cumsum(sizes)[:-1].tolist()
    return jnp.split(z, offs, axis=-1)


def gla_chunked(q, k, v, log_a):
    B, T, H, dk = q.shape
    dv = v.shape[-1]
    N = T // CHUNK

    def chunk(z):
        return z.astype(jnp.float32).reshape(B, N, CHUNK, H, z.shape[-1]).transpose(1, 0, 3, 2, 4)

    qc, kc, vc, lac = chunk(q), chunk(k), chunk(v), chunk(log_a)
    b = jnp.cumsum(lac, axis=3)
    b_last = b[:, :, :, -1:, :]
    q_dec = qc * (dk ** -0.5) * jnp.exp(b)
    k_intra = kc * jnp.exp(-b)
    k_state = kc * jnp.exp(b_last - b)
    decay = jnp.exp(b_last[:, :, :, 0, :])

    causal = jnp.tril(jnp.ones((CHUNK, CHUNK), dtype=bool))
    scores = jnp.einsum('nbhcd,nbhsd->nbhcs', q_dec, k_intra)
    scores = jnp.where(causal, scores, 0.0)
    o_intra = jnp.einsum('nbhcs,nbhse->nbhce', scores, vc)

    def step(S, inp):
        qd, ks, vv, dec = inp
        o = jnp.einsum('bhcd,bhde->bhce', qd, S)
        S = dec[..., None] * S + jnp.einsum('bhcd,bhce->bhde', ks, vv)
        return S, o

    S0 = jnp.zeros((B, H, dk, dv), jnp.float32)
    _, o_inter = lax.scan(step, S0, (q_dec, k_state, vc, decay))
    o = o_intra + o_inter
    return o.transpose(1, 0, 3, 2, 4).reshape(B, T, H, dv)


def stick_breaking_attention(q, k, v):
    T = q.shape[2]
    d = q.shape[-1]
    scale = d ** -0.5
    qf, kf, vf = q.astype(jnp.float32), k.astype(jnp.float32), v.astype(jnp.float32)
    outs = []
    for i in range(T // SB_BLOCK):
        q0, kend = i * SB_BLOCK, (i + 1) * SB_BLOCK
        qb = qf[:, :, q0:kend]
        kb, vb = kf[:, :, :kend], vf[:, :, :kend]
        z = jnp.einsum('bhqd,bhkd->bhqk', qb, kb) * scale
        t_idx = q0 + jnp.arange(SB_BLOCK)
        s_idx = jnp.arange(kend)
        mask = s_idx[None, :] < t_idx[:, None]
        log_beta = jax.nn.log_sigmoid(z)
        log_1mb = jnp.where(mask, jax.nn.log_sigmoid(-z), 0.0)
        suffix = lax.cumsum(log_1mb, axis=3, reverse=True) - log_1mb
        A = jnp.where(mask, jnp.exp(log_beta + suffix), 0.0)
        outs.append(jnp.einsum('bhqk,bhkd->bhqd', A, vb))
    return jnp.concatenate(outs, axis=2)


def mixer_block(h, w_in, w_gate_up, b_gate, gla_norm, w_branch_gla, w_branch_sb, w_out):
    B, T, _ = h.shape
    z = h @ w_in
    (gq, gk, gv, g_lr, g_out, sq, sk, sv, gate_a, gate_b) = split_cols(z, IN_SPLITS)

    log_a = jax.nn.log_sigmoid((g_lr @ w_gate_up + b_gate).astype(jnp.float32)) / GATE_TAU
    o_a = gla_chunked(gq.reshape(B, T, GLA_HEADS, GLA_DK),
                      gk.reshape(B, T, GLA_HEADS, GLA_DK),
                      gv.reshape(B, T, GLA_HEADS, GLA_DV),
                      log_a.reshape(B, T, GLA_HEADS, GLA_DK))
    o_a = rmsnorm(o_a, gla_norm).reshape(B, T, GLA_VAL_W).astype(h.dtype)
    o_a = o_a * jax.nn.silu(g_out)
    y_a = o_a @ w_branch_gla

    def heads(t):
        return t.reshape(B, T, SB_HEADS, SB_DH).transpose(0, 2, 1, 3)
    o_b = stick_breaking_attention(heads(sq), heads(sk), heads(sv))
    o_b = o_b.transpose(0, 2, 1, 3).reshape(B, T, SB_W).astype(h.dtype)
    y_b = o_b @ w_branch_sb

    y = jax.nn.sigmoid(gate_a) * y_a + jax.nn.sigmoid(gate_b) * y_b
    return y @ w_out


def setup_inputs(seed: int = 0) -> dict:
    key = jax.random.key(seed)
    ks = jax.random.split(key, 20)

    def w(k, shape, fan_in):
        return jax.random.normal(k, shape, jnp.float32) * (fan_in ** -0.5)

    def gain(k, n):
        return 1.0 + 0.02 * jax.random.normal(k, (DEPTH, n), jnp.float32)

    return {
        "x": jax.random.normal(ks[0], (BATCH, SEQ, D_MODEL), jnp.float32),
        "p": jax.random.normal(ks[1], (DEPTH, BATCH, SEQ, PLE_DIM), jnp.float32),
        "norm_mix_pre": gain(ks[2], D_MODEL),
        "norm_mix_post": gain(ks[3], D_MODEL),
        "w_in": w(ks[4], (DEPTH, D_MODEL, IN_W), D_MODEL),
        "w_gate_up": w(ks[5], (DEPTH, GATE_RANK, GLA_KEY_W), GATE_RANK),
        "b_gate": 0.1 * jax.random.normal(ks[6], (DEPTH, GLA_KEY_W), jnp.float32),
        "gla_norm": gain(ks[7], GLA_DV),
        "w_branch_gla": w(ks[8], (DEPTH, GLA_VAL_W, D_MODEL), GLA_VAL_W),
        "w_branch_sb": w(ks[9], (DEPTH, SB_W, D_MODEL), SB_W),
        "w_out": w(ks[10], (DEPTH, D_MODEL, D_MODEL), D_MODEL),
        "norm_mlp_pre": gain(ks[11], D_MODEL),
        "norm_mlp_post": gain(ks[12], D_MODEL),
        "w_mlp_up": w(ks[13], (DEPTH, D_MODEL, D_FF), D_MODEL),
        "w_mlp_down": w(ks[14], (DEPTH, D_FF, D_MODEL), D_FF),
        "norm_ple": gain(ks[15], D_MODEL),
        "w_ple_gate": w(ks[16], (DEPTH, D_MODEL, D_MODEL), D_MODEL),
        "w_ple_proj": w(ks[17], (DEPTH, PLE_DIM, D_MODEL), PLE_DIM),
    }


def reference(x, p, norm_mix_pre, norm_mix_post, w_in, w_gate_up, b_gate, gla_norm,
              w_branch_gla, w_branch_sb, w_out, norm_mlp_pre, norm_mlp_post,
              w_mlp_up, w_mlp_down, norm_ple, w_ple_gate, w_ple_proj):
    h = x
    for i in range(DEPTH):
        u = rmsnorm(h, norm_mix_pre[i])
        m = mixer_block(u, w_in[i], w_gate_up[i], b_gate[i], gla_norm[i],
                        w_branch_gla[i], w_branch_sb[i], w_out[i])
        h = h + rmsnorm(m, norm_mix_post[i])
        u = rmsnorm(h, norm_mlp_pre[i])
        f = jnp.square(jax.nn.relu(u @ w_mlp_up[i])) @ w_mlp_down[i]
        h = h + rmsnorm(f, norm_mlp_post[i])
        e = p[i] @ w_ple_proj[i]
        g = jax.nn.sigmoid(rmsnorm(h, norm_ple[i]) @ w_ple_gate[i])
        h = h + g * e
    return h
```

```python
import numpy as np
import concourse.bass as bass
import concourse.mybir as mybir
from concourse.bass_utils import run_bass_kernel_spmd
from contextlib import ExitStack

F32 = mybir.dt.float32
BF16 = mybir.dt.bfloat16
AF = mybir.ActivationFunctionType
ALU = mybir.AluOpType
AX = mybir.AxisListType

class Buf:
    def __init__(self, name, t, space=None, lo=0, hi=1):
        self.name = name
        self.t = t
        self.space = space if space is not None else name
        self.lo = lo
        self.hi = hi
        self.entries = {}

    def __getitem__(self, k):
        return self.t[k]


class Op:
    __slots__ = ("eng", "fn", "deps", "dma", "idx", "lidx", "sem", "val", "milestone", "mcount", "waits")

    def __init__(self, eng, fn, dma):
        self.eng = eng
        self.fn = fn
        self.deps = {}
        self.dma = dma
        self.sem = None
        self.val = 0
        self.milestone = False
        self.mcount = 0
        self.waits = None


class Prog:
    ENGS = ("pe", "act", "dve", "pool", "sp")
    NDMA = 12

    def __init__(self, nc, spaces=None, plan=False):
        self.nc = nc
        self.plan = plan
        self.ops = {e: [] for e in self.ENGS}
        self.nops = 0
        self.spaces = spaces if spaces is not None else {}
        self.panel_list = []
        self.dma_count = {"sp": 0, "pool": 0, "act": 0}
        self.dma_ops = {"sp": [], "pool": [], "act": []}
        self.panel_log = []

    def buf(self, name, t, space=None, lo=0, hi=1):
        b = Buf(name, t, space, lo, hi)
        self.spaces.setdefault(b.space, []).append(b)
        return b

    def _conf_entries(self, b, k):
        for k2, e in b.entries.items():
            if k is None or k2 is None or k2 == k:
                yield e
        sp = self.spaces[b.space]
        if len(sp) > 1:
            for b2 in sp:
                if b2 is not b and b2.lo < b.hi and b.lo < b2.hi:
                    for e in b2.entries.values():
                        yield e

    def add(self, eng, fn, reads=(), writes=(), dma=False):
        if self.plan:
            return None
        op = Op(eng, fn, dma)
        op.idx = self.nops
        self.nops += 1
        op.lidx = len(self.ops[eng])
        deps = op.deps
        for (b, k) in reads:
            for e in self._conf_entries(b, k):
                if e[0] is not None:
                    deps[e[0]] = True
        for (b, k) in writes:
            for e in self._conf_entries(b, k):
                if e[0] is not None and e[0] not in deps:
                    deps[e[0]] = False
                for r in e[1].values():
                    if r not in deps:
                        deps[r] = False
                for r in e[2]:
                    if r not in deps:
                        deps[r] = False
        for (b, k) in reads:
            e = b.entries.get(k)
            if e is None:
                e = b.entries[k] = [None, {}, []]
            if dma:
                e[2].append(op)
            else:
                e[1][eng] = op
        for (b, k) in writes:
            if k is None:
                for kk in list(b.entries.keys()):
                    if kk is not None:
                        del b.entries[kk]
            b.entries[k] = [op, {}, []]
        deps.pop(op, None)
        if dma:
            q = eng
            n = self.dma_count[q]
            self.dma_count[q] += 1
            op.sem = (q, n % self.NDMA)
            op.val = 16 * (n // self.NDMA + 1)
            if n >= self.NDMA:
                deps[self.dma_ops[q][n - self.NDMA]] = True
            self.dma_ops[q].append(op)
        self.ops[eng].append(op)
        return op

    def emit(self, es):
        nc = self.nc
        engobj = {"pe": nc.tensor, "act": nc.scalar, "dve": nc.vector, "pool": nc.gpsimd, "sp": nc.sync}
        esem = {e: es.enter_context(nc.semaphore("sem_" + e)) for e in ("pe", "act", "dve", "pool")}
        dsem = {}
        for q in ("sp", "pool", "act"):
            if self.dma_count[q]:
                for i in range(min(self.NDMA, self.dma_count[q])):
                    dsem[(q, i)] = es.enter_context(nc.semaphore("dsem_%s_%d" % (q, i)))
        for E in self.ENGS:
            seen = {}
            seend = {}
            for op in self.ops[E]:
                w = []
                for p, raw in op.deps.items():
                    if p.dma:
                        if seend.get(p.sem, 0) >= p.val:
                            continue
                        seend[p.sem] = p.val
                        w.append(p)
                    else:
                        if p.eng == E:
                            if E == "pe" or not raw:
                                continue
                        if seen.get(p.eng, -1) >= p.lidx:
                            continue
                        seen[p.eng] = p.lidx
                        p.milestone = True
                        w.append(p)
                op.waits = w
        for E in self.ENGS:
            c = 0
            for op in self.ops[E]:
                if op.milestone and not op.dma:
                    c += 1
                op.mcount = c
        nwait = 0
        for E in self.ENGS:
            eo = engobj[E]
            for op in self.ops[E]:
                best = {}
                for p in op.waits:
                    if p.dma:
                        key = dsem[p.sem]
                        v = p.val
                    else:
                        key = esem[p.eng]
                        v = p.mcount
                    if best.get(key, (0, None))[0] < v:
                        best[key] = (v, key)
                for v, key in best.values():
                    eo.wait_ge(key, v)
                    nwait += 1
                inst = op.fn()
                if op.dma:
                    inst.then_inc(dsem[op.sem], 16)
                elif op.milestone:
                    inst.then_inc(esem[E], 1)
        for q in ("sp", "pool", "act"):
            n = self.dma_count[q]
            for i in range(min(self.NDMA, n)):
                cnt = (n - 1 - i) // self.NDMA + 1
                nc.sync.wait_ge(dsem[(q, i)], 16 * cnt)
        return nwait

NT = 8
TT = 512
EPS = 1e-6
W_IN_ORDER_OTHER = [1, 2, 3, 8, 9, 10, 11]
DEBUG = {}


def build(ntiles=NT, stop_after=None, dbg=()):
    nc = bass.Bass("TRN2", target_bir_lowering=False)

    def dram(name, shape, dt=F32, kind="ExternalInput"):
        return nc.dram_tensor(name, shape, dt, kind=kind).ap()

    xs = dram("xs", [NT, 4, 128, 2048])
    pp = dram("pp", [4, 4, 128, 256])
    w_in_p = dram("w_in_p", [20, 128, 8192])
    w_glr = dram("w_glr", [128, 256])
    wg1 = dram("wg1", [32, 512])
    w_bg = dram("w_bg", [4, 128, 4096])
    w_bs = dram("w_bs", [4, 128, 4096])
    w_out_p = dram("w_out_p", [4, 128, 8192])
    w_up_p = dram("w_up_p", [16, 128, 8192])
    w_dn_p = dram("w_dn_p", [4, 4, 128, 8192])
    w_pg_p = dram("w_pg_p", [4, 128, 8192])
    w_pp_p = dram("w_pp_p", [4, 128, 1024])
    gfm_d = dram("gfm", [128, 64])
    gtm_d = dram("gtm", [2, 128, 2048])
    consts_d = dram("consts", [128, 640])
    y = dram("y", [4, 4, 128, 2048], F32, "ExternalOutput")
    kcache = dram("kcache", [8, 128, NT * TT], BF16, "Internal")
    vcache = dram("vcache", [NT * 4, 128, 1024], BF16, "Internal")
    dbg_out = {}

    with ExitStack() as es:
        e = es.enter_context
        spaces = {}

        def sb(name, shape, dt):
            t = e(nc.sbuf_tensor(name, shape, dt))
            b = Buf(name, t)
            spaces.setdefault(b.space, []).append(b)
            return b

        ring = [sb("ring%d" % i, [128, 8192], BF16) for i in range(3)]
        xh = [sb("xh%d" % i, [128, 2048], F32) for i in range(4)]
        xin = [sb("xin%d" % i, [128, 2048], F32) for i in range(2)]
        xn = [sb("xn%d" % i, [128, 2048], BF16) for i in range(2)]
        uT = sb("uT", [128, 16, 512], BF16)
        Tst = sb("Tst", [128, 4, 256], F32)
        S_bf = sb("S_bf", [128, 4, 256], BF16)
        decs = sb("decs", [128, 5, 4], F32)
        gfm = sb("gfm_sb", [128, 64], F32)
        cst = sb("cst", [128, 640], F32)
        idb = sb("idb", [128, 128], BF16)
        onesb = sb("onesb", [128, 128], BF16)
        zerosb = sb("zerosb", [128, 128], BF16)
        negonesb = sb("negonesb", [128, 128], BF16)
        negUb = sb("negUb", [128, 128], BF16)
        wglr = sb("wglr", [128, 256], BF16)
        wg1sb = sb("wg1sb", [32, 512], F32)
        glrT1 = sb("glrT1", [32, 512], F32)
        sst = sb("sst", [128, 16], F32)
        Ut = e(nc.sbuf_tensor("U", [128, 36864], BF16))
        ps = []
        dbl = []
        for k in range(4):
            t = e(nc.psum_tensor("dbl%d" % k, [128, 1024], F32))
            db = Buf("dbl%d" % k, t, "dbl%d" % k, 0, 4096)
            spaces.setdefault(db.space, []).append(db)
            dbl.append(db)
            for i in range(2):
                b = Buf("ps%d" % (2 * k + i), t[:, i * 512:(i + 1) * 512], "dbl%d" % k, i * 2048, (i + 1) * 2048)
                spaces[db.space].append(b)
                ps.append(b)
        kcb = Buf("kcache", None)
        vcb = Buf("vcache", None)
        spaces["kcache"] = [kcb]
        spaces["vcache"] = [vcb]

        def carve(name, off_kb, shape, dt):
            nel = int(np.prod(shape[1:]))
            nbytes = nel * (4 if dt == F32 else 2)
            lo = int(off_kb * 1024)
            assert lo + nbytes <= 36864 * 2, name
            v = Ut[:, lo // 2: (lo + nbytes) // 2]
            if dt == F32:
                v = v.bitcast(F32)
            if len(shape) == 3:
                v = v.rearrange("p (a b) -> p a b", a=shape[1])
            elif len(shape) == 4:
                v = v.rearrange("p (a b c) -> p a b c", a=shape[1], b=shape[2])
            b = Buf(name, v, "U", lo, lo + nbytes)
            spaces.setdefault("U", []).append(b)
            return b

        sp_tok = carve("sp_tok", 0, [128, 4, 512], F32)
        eb = carve("eb", 8, [128, 4, 512], F32)
        enb = carve("enb", 16, [128, 4, 512], F32)
        kiT = carve("kiT", 24, [128, 4, 512], BF16)
        qdT = carve("qdT", 52, [128, 4, 512], BF16)
        gv_tok = carve("gv_tok", 28, [128, 4, 1024], BF16)
        kst = carve("kst", 36, [128, 8, 512], BF16)
        vst = carve("vst", 44, [128, 4, 1024], BF16)
        silu_g = carve("silu_g", 56, [128, 8, 512], BF16)
        qT = carve("qT", 64, [128, 8, 512], BF16)
        uT_alt = carve("uT_alt", 56, [128, 16, 512], BF16)
        o_aT = carve("o_aT", 0, [128, 8, 512], BF16)
        k_tok = carve("k_tok", 8, [128, 4, 512], BF16)
        scm = carve("scm", 12, [128, 4, 128], BF16)
        sq = carve("sq", 13, [128, 8, 128], BF16)
        rstdg = carve("rstdg", 16, [128, 4, 128], F32)
        tmp_o = carve("tmp_o", 18, [128, 8, 128], F32)
        o_bT = carve("o_bT", 56, [128, 8, 512], BF16)
        NH = 4
        Ep = [carve("Ep%d" % i, 8 + 22 * i + 0, [128, 2, 512], F32) for i in range(2)]
        SPp = [carve("SPp%d" % i, 8 + 22 * i + 4, [128, 2, 512], BF16) for i in range(2)]
        SPsp = [carve("SPsp%d" % i, 8 + 22 * i + 6, [128, 2, 512], BF16) for i in range(2)]
        ECp = [carve("ECp%d" % i, 8 + 22 * i + 8, [128, 2, 512], F32) for i in range(2)]
        Ap = [carve("Ap%d" % i, 8 + 22 * i + 12, [128, 2, 512], BF16) for i in range(2)]
        Kt = [[carve("Kt%d%d" % (i, j), 8 + 22 * (i // 2) + 14 + 4 * (i % 2) + j, [128, 512], BF16) for j in range(2)] for i in range(NH)]
        Vt = [[carve("Vt%d%d" % (i, j), 8 + 22 * (i // 2) + 16 + 4 * (i % 2) + j, [128, 4, 128], BF16) for j in range(2)] for i in range(NH)]
        yT = carve("yT", 8, [128, 16, 512], BF16)
        sga = [carve("sga%d" % i, 24 + 2 * i, [128, 512], F32) for i in range(4)]
        sgb = [carve("sgb%d" % i, 32 + 2 * i, [128, 512], F32) for i in range(4)]
        m_sb = [carve("m_sb0", 40, [128, 2048], F32), carve("m_sb1", 48, [128, 2048], F32),
                carve("m_sb2", 0, [128, 2048], F32), carve("m_sb3", 56, [128, 2048], F32)]
        gbc = carve("gbc", 64, [128, 2048], F32)
        aT = carve("aT", 0, [128, 32, 512], BF16)
        f_sb = [carve("f_sb%d" % i, 32 + 8 * i, [128, 2048], F32) for i in range(4)]
        rtmp = [carve("rtmp%d" % i, 64 + 2 * i, [128, 512], F32) for i in range(2)]
        gbc2 = carve("gbc2", 0, [128, 2048], F32)
        p_f = carve("p_f", 0, [128, 4, 256], F32)
        p_b = carve("p_b", 4, [128, 4, 256], BF16)
        pT = carve("pT", 6, [128, 2, 512], BF16)
        sgp = [carve("sgp%d" % i, 8 + 2 * i, [128, 512], F32) for i in range(4)]

        def run(P):
            plan = P.plan
            ring_state = {"n": 0, "issued": 0}

            def panel(ap, nel):
                i = ring_state["n"]
                ring_state["n"] += 1
                if plan:
                    P.panel_log.append((ap, nel))
                    return ring[i % 3]
                while ring_state["issued"] < min(i + 3, len(P.panel_list)):
                    j = ring_state["issued"]
                    apj, nelj = P.panel_list[j]
                    rb = ring[j % 3]
                    P.add("pool", lambda apj=apj, nelj=nelj, rb=rb: nc.gpsimd.dma_start(out=rb[:, 0:nelj], in_=apj),
                          writes=[(rb, None)], dma=True)
                    ring_state["issued"] += 1
                return ring[i % 3]

            rot = {"d": 0, "ev": 0, "nb": 4}

            def dbank():
                rot["d"] = (rot["d"] + 1) % rot["nb"]
                return ps[rot["d"]]

            def interleave(*gens):
                alive = list(gens)
                while alive:
                    for g in list(alive):
                        try:
                            next(g)
                        except StopIteration:
                            alive.remove(g)

            def evac(out_fn, in_fn, reads, writes, eng=None):
                if eng is None:
                    rot["ev"] ^= 1
                    eng = "act" if rot["ev"] else "dve"
                if eng == "act":
                    P.add("act", lambda: nc.scalar.copy(out=out_fn(), in_=in_fn()), reads=reads, writes=writes)
                else:
                    P.add("dve", lambda: nc.vector.tensor_copy(out=out_fn(), in_=in_fn()), reads=reads, writes=writes)

            def dump(name, buf, apfn, shape):
                if name in dbg and not plan:
                    if name not in dbg_out:
                        dbg_out[name] = dram("dbg_" + name, shape, F32, "ExternalOutput")
                    stage = e(nc.sbuf_tensor("dbgs_" + name, shape, F32))
                    sbf = Buf("dbgs_" + name, stage)
                    spaces[sbf.space] = [sbf]
                    P.add("dve", lambda: nc.vector.tensor_copy(out=stage[:], in_=apfn()), reads=[(buf, None)], writes=[(sbf, None)])
                    P.add("sp", lambda: nc.sync.dma_start(out=dbg_out[name], in_=stage[:]), reads=[(sbf, None)], dma=True)

            P.add("sp", lambda: nc.sync.dma_start(out=gfm[:], in_=gfm_d), writes=[(gfm, None)], dma=True)
            P.add("sp", lambda: nc.sync.dma_start(out=cst[:], in_=consts_d), writes=[(cst, None)], dma=True)
            P.add("sp", lambda: nc.sync.dma_start(out=wg1sb[:], in_=wg1), writes=[(wg1sb, None)], dma=True)
            P.add("pool", lambda: nc.gpsimd.dma_start(out=wglr[:], in_=w_glr), writes=[(wglr, None)], dma=True)
            P.add("dve", lambda: nc.vector.tensor_copy(out=idb[:], in_=cst[:, 0:128]), reads=[(cst, None)], writes=[(idb, None)])
            P.add("dve", lambda: nc.vector.memset(onesb[:], 1.0), writes=[(onesb, None)])
            P.add("dve", lambda: nc.vector.memset(zerosb[:], 0.0), writes=[(zerosb, None)])
            P.add("dve", lambda: nc.vector.memset(negonesb[:], -1.0), writes=[(negonesb, None)])
            P.add("dve", lambda: nc.vector.tensor_copy(out=negUb[:], in_=cst[:, 256:384]), reads=[(cst, None)], writes=[(negUb, None)])
            P.add("dve", lambda: nc.vector.memset(glrT1[:], 1.0), writes=[(glrT1, None)])
            P.add("dve", lambda: nc.vector.memset(Tst[:], 0.0), writes=[(Tst, None)])
            P.add("dve", lambda: nc.vector.memset(decs[:], 1.0), writes=[(decs, None)])
            maskS = lambda: cst[:, 128:256]
            negU = lambda: cst[:, 256:384]
            triN = lambda: cst[:, 384:512]
            maskC = lambda: cst[:, 512:640]

            def norm_transpose_g(src_buf, src_fn, gcol, blk, dst=None, banks=None):
                dst = uT if dst is None else dst
                xnb = xn[blk % 2]
                c0 = 3 * (blk % 2)
                P.add("act", lambda: nc.scalar.activation(out=xnb[:], in_=src_fn(), func=AF.Square, accum_out=sst[:, c0:c0 + 1]),
                      reads=[(src_buf, None)], writes=[(xnb, None), (sst, c0)])
                P.add("act", lambda: nc.scalar.activation(out=sst[:, c0 + 1:c0 + 2], in_=sst[:, c0:c0 + 1], func=AF.Ln, scale=1.0 / 2048, bias=EPS),
                      reads=[(sst, c0)], writes=[(sst, c0 + 1)])
                P.add("act", lambda: nc.scalar.activation(out=sst[:, c0 + 2:c0 + 3], in_=sst[:, c0 + 1:c0 + 2], func=AF.Exp, scale=-0.5),
                      reads=[(sst, c0 + 1)], writes=[(sst, c0 + 2)])
                P.add("act", lambda: nc.scalar.activation(out=xnb[:], in_=src_fn(), func=AF.Copy, scale=sst[:, c0 + 2:c0 + 3]),
                      reads=[(src_buf, None), (sst, c0 + 2)], writes=[(xnb, None)])
                yield
                for half in range(2):
                    bank = ps[2 * (blk % 2) + half] if banks is None else banks[half]

                    def tr(half=half, bank=bank):
                        last = None
                        for k in range(8):
                            kc = half * 8 + k
                            last = nc.tensor.transpose(out=bank[:].bitcast(BF16)[:, k * 128:(k + 1) * 128],
                                                       in_=xnb[:, kc * 128:(kc + 1) * 128], identity=idb[:])
                        return last
                    P.add("pe", tr, reads=[(xnb, None), (idb, None)], writes=[(bank, None)])
                    P.add("dve", lambda half=half, bank=bank: nc.vector.tensor_tensor(
                        out=dst[:, half * 8:(half + 1) * 8, blk * 128:(blk + 1) * 128],
                        in0=bank[:].bitcast(BF16).rearrange("p (a b) -> p a b", a=8),
                        in1=gfm[:, gcol + half * 8: gcol + half * 8 + 8].unsqueeze(2).to_broadcast([128, 8, 128]),
                        op=ALU.mult), reads=[(bank, None), (gfm, None)], writes=[(dst, blk)])
                    yield

            def norm_transpose(src_buf, src_fn, gcol, blk):
                for _ in norm_transpose_g(src_buf, src_fn, gcol, blk):
                    pass

            def s0_gen(tau_):
                dst = uT if tau_ % 2 == 1 else uT_alt
                for blk in range(4):
                    xb_ = xin[blk % 2]
                    P.add("sp", lambda tau_=tau_, blk=blk, xb_=xb_: nc.sync.dma_start(out=xb_[:], in_=xs[tau_, blk]),
                          writes=[(xb_, None)], dma=True)
                    yield
                    yield from norm_transpose_g(xb_, lambda xb_=xb_: xb_[:], 0, blk, dst=dst, banks=(ps[2], ps[3]))

            def fm_group(rb, ncol0, kcs, rhs_fn, rhs_reads, bank, ncols=512, m=128, out_fn=None):
                def f():
                    last = None
                    o = out_fn() if out_fn else bank[:, 0:ncols]
                    for i, kc in enumerate(kcs):
                        last = nc.tensor.matmul(o, rb[:, kc * 512 + ncol0: kc * 512 + ncol0 + m], rhs_fn(kc),
                                                start=(i == 0), stop=(i == len(kcs) - 1))
                    return last
                P.add("pe", f, reads=[(rb, None)] + rhs_reads, writes=[(bank, None)])

            def tm_group(rb, kcs, lhs_fn, lhs_reads, bank, start=True, stop=True):
                def f():
                    last = None
                    for i, kc in enumerate(kcs):
                        last = nc.tensor.matmul(bank[:], lhs_fn(kc), rb[:, kc * 512:(kc + 1) * 512],
                                                start=(start and i == 0), stop=(stop and i == len(kcs) - 1))
                    return last
                P.add("pe", f, reads=[(rb, None)] + lhs_reads, writes=[(bank, None)])

            K16 = list(range(16))

            for _ in s0_gen(0):
                pass
            for tau in range(ntiles):
                own = (tau % 2 == 1)
                oj = tau // 2
                UT = uT if own else uT_alt
                def glr_mm(UT=UT):
                    last = None
                    for kc in range(16):
                        last = nc.tensor.matmul(ps[2][0:16, :], wglr[:, kc * 16:(kc + 1) * 16], UT[:, kc, :],
                                                start=(kc == 0), stop=(kc == 15))
                    return last
                P.add("pe", glr_mm, reads=[(wglr, None), (UT, None)], writes=[(ps[2], None)])
                P.add("act", lambda: nc.scalar.copy(out=glrT1[0:16, :], in_=ps[2][0:16, :]), reads=[(ps[2], None)], writes=[(glrT1, None)])
                for blk in range(4):
                    bank = ps[3 if blk % 2 == 0 else 2]
                    P.add("pe", lambda blk=blk, bank=bank: nc.tensor.matmul(bank[:], glrT1[0:32, blk * 128:(blk + 1) * 128], wg1sb[0:32, :], start=True, stop=True),
                          reads=[(glrT1, None), (wg1sb, None)], writes=[(bank, None)])
                    P.add("act", lambda blk=blk, bank=bank: nc.scalar.activation(out=sp_tok[:, blk, :], in_=bank[:], func=AF.Exp, scale=-1.0),
                          reads=[(bank, None)], writes=[(sp_tok, blk)])
                    P.add("act", lambda blk=blk: nc.scalar.activation(out=sp_tok[:, blk, :], in_=sp_tok[:, blk, :], func=AF.Ln, bias=1.0),
                          reads=[(sp_tok, blk)], writes=[(sp_tok, blk)])
                for h in range(4):
                    bank = ps[4 + h]

                    def bt_mm(h=h, bank=bank):
                        last = None
                        for blk in range(4):
                            last = nc.tensor.matmul(bank[:, blk * 128:(blk + 1) * 128], sp_tok[:, blk, h * 128:(h + 1) * 128], triN(), start=True, stop=True)
                        return last
                    P.add("pe", bt_mm, reads=[(sp_tok, None), (cst, None)], writes=[(bank, None)])
                    P.add("act", lambda h=h, bank=bank: nc.scalar.activation(out=eb[:, h, :], in_=bank[:], func=AF.Exp),
                          reads=[(bank, None)], writes=[(eb, h)])
                    P.add("act", lambda h=h, bank=bank: nc.scalar.activation(out=enb[:, h, :], in_=bank[:], func=AF.Exp, scale=-1.0),
                          reads=[(bank, None)], writes=[(enb, h)])
                    P.add("dve", lambda h=h: nc.vector.tensor_copy(out=decs[:, 1:5, h], in_=eb[:, h, 127:512:128]),
                          reads=[(eb, h)], writes=[(decs, None)])
                if tau == 1:
                    dump("eb1", eb, lambda: eb[:, 0, :], [128, 512])
                rb = panel(w_in_p[1], 8192)
                for h in range(4):
                    bank = dbank()
                    fm_group(rb, h * 128, K16, lambda kc, UT=UT: UT[:, kc, :], [(UT, None)], bank)
                    P.add("dve", lambda h=h, bank=bank: nc.vector.tensor_tensor(out=kiT[:, h, :], in0=bank[:], in1=enb[:, h, :], op=ALU.mult),
                          reads=[(bank, None), (enb, h)], writes=[(kiT, h)])
                if own:
                    rb = panel(w_in_p[0], 8192)
                    for h in range(4):
                        bank = dbank()
                        fm_group(rb, h * 128, K16, lambda kc, UT=UT: UT[:, kc, :], [(UT, None)], bank)
                        P.add("dve", lambda h=h, bank=bank: nc.vector.scalar_tensor_tensor(
                            out=qdT[:, h, :], in0=bank[:], scalar=float(128 ** -0.5), in1=eb[:, h, :], op0=ALU.mult, op1=ALU.mult),
                            reads=[(bank, None), (eb, h)], writes=[(qdT, h)])
                for half in range(2):
                    rb = panel(w_in_p[2 + half], 8192)
                    for blk in range(4):
                        bank = dbank()
                        tm_group(rb, K16, lambda kc, blk=blk, UT=UT: UT[:, kc, blk * 128:(blk + 1) * 128], [(UT, None)], bank)
                        evac(lambda blk=blk, half=half: gv_tok[:, blk, half * 512:(half + 1) * 512], lambda bank=bank: bank[:],
                             [(bank, None)], [(gv_tok, (blk, half))])
                if own:
                    for half in range(2):
                        rb = panel(w_in_p[4 + half], 8192)
                        for cc in range(4):
                            c8 = half * 4 + cc
                            bank = dbank()
                            fm_group(rb, cc * 128, K16, lambda kc, UT=UT: UT[:, kc, :], [(UT, None)], bank)
                            P.add("act", lambda c8=c8, bank=bank: nc.scalar.activation(out=silu_g[:, c8, :], in_=bank[:], func=AF.Silu),
                                  reads=[(bank, None)], writes=[(silu_g, c8)])

                def dense_gen(tau=tau, own=own):
                    for half in range(2):
                        rb = panel(w_in_p[8 + half], 8192)
                        for hh in range(4):
                            h = half * 4 + hh
                            bank = dbank()
                            fm_group(rb, hh * 128, K16, lambda kc, UT=UT: UT[:, kc, :], [(UT, None)], bank)
                            evac(lambda h=h: kst[:, h, :], lambda bank=bank: bank[:], [(bank, None)], [(kst, h)])
                            yield
                    P.add("sp", lambda tau=tau: nc.sync.dma_start(out=kcache[:, :, tau * 512:(tau + 1) * 512].rearrange("h d t -> d h t"), in_=kst[:, :, :]),
                          reads=[(kst, None)], writes=[(kcb, tau)], dma=True)
                    for half in range(2):
                        rb = panel(w_in_p[10 + half], 8192)
                        for blk in range(4):
                            bank = dbank()
                            tm_group(rb, K16, lambda kc, blk=blk, UT=UT: UT[:, kc, blk * 128:(blk + 1) * 128], [(UT, None)], bank)
                            evac(lambda blk=blk, half=half: vst[:, blk, half * 512:(half + 1) * 512], lambda bank=bank: bank[:],
                                 [(bank, None)], [(vst, (blk, half))])
                            yield
                    P.add("sp", lambda tau=tau: nc.sync.dma_start(out=vcache[tau * 4:(tau + 1) * 4].rearrange("b t c -> t b c"), in_=vst[:, :, :]),
                          reads=[(vst, None)], writes=[(vcb, tau)], dma=True)
                    if own:
                        for half in range(2):
                            rb = panel(w_in_p[6 + half], 8192)
                            for hh in range(4):
                                h = half * 4 + hh
                                bank = dbank()
                                fm_group(rb, hh * 128, K16, lambda kc, UT=UT: UT[:, kc, :], [(UT, None)], bank)
                                evac(lambda h=h: qT[:, h, :], lambda bank=bank: bank[:], [(bank, None)], [(qT, h)])
                                yield

                def gla_gen(tau=tau, own=own):
                    for blk in range(4):
                        tb = ps[4]

                        def ktr(blk=blk, tb=tb):
                            last = None
                            for h in range(4):
                                last = nc.tensor.transpose(out=tb[:].bitcast(BF16)[:, h * 128:(h + 1) * 128],
                                                           in_=kiT[:, h, blk * 128:(blk + 1) * 128], identity=idb[:])
                            return last
                        P.add("pe", ktr, reads=[(kiT, None), (idb, None)], writes=[(tb, None)])
                        P.add("dve", lambda blk=blk, tb=tb: nc.vector.tensor_copy(out=k_tok[:, blk, :], in_=tb[:].bitcast(BF16)[:, 0:512]),
                              reads=[(tb, None)], writes=[(k_tok, blk)])
                        yield
                        if own:
                            for h in range(4):
                                P.add("dve", lambda h=h, blk=blk: nc.vector.tensor_scalar(out=S_bf[:, h, :], in0=Tst[:, h, :], scalar1=decs[:, blk, h:h + 1], scalar2=None, op0=ALU.mult),
                                      reads=[(Tst, h), (decs, None)], writes=[(S_bf, h)])
                            scb = ps[4]

                            def sc_mm(blk=blk, scb=scb):
                                last = None
                                for h in range(4):
                                    last = nc.tensor.matmul(scb[:, h * 128:(h + 1) * 128], kiT[:, h, blk * 128:(blk + 1) * 128],
                                                            qdT[:, h, blk * 128:(blk + 1) * 128], start=True, stop=True)
                                return last
                            P.add("pe", sc_mm, reads=[(kiT, None), (qdT, None)], writes=[(scb, None)])
                            P.add("dve", lambda scb=scb: nc.vector.tensor_tensor(
                                out=scm[:, :, :], in0=scb[:].rearrange("p (h c) -> p h c", h=4),
                                in1=maskC().unsqueeze(1).to_broadcast([128, 4, 128]), op=ALU.mult),
                                reads=[(scb, None), (cst, None)], writes=[(scm, None)])
                            yield
                            for b2 in range(2):
                                ob = ps[5 + b2]

                                def o_mm(b2=b2, ob=ob, blk=blk):
                                    last = None
                                    for i in range(4):
                                        idx = b2 * 4 + i
                                        h, ec = idx // 2, idx % 2
                                        o = ob[:, i * 128:(i + 1) * 128]
                                        nc.tensor.matmul(o, S_bf[:, h, ec * 128:(ec + 1) * 128], qdT[:, h, blk * 128:(blk + 1) * 128], start=True, stop=False)
                                        last = nc.tensor.matmul(o, gv_tok[:, blk, h * 256 + ec * 128: h * 256 + (ec + 1) * 128], scm[:, h, :], start=False, stop=True)
                                    return last
                                P.add("pe", o_mm, reads=[(S_bf, None), (qdT, None), (gv_tok, None), (scm, None)], writes=[(ob, None)])
                                P.add("act", lambda b2=b2, ob=ob: nc.scalar.activation(out=sq[:, b2 * 4:(b2 + 1) * 4, :], in_=ob[:].rearrange("p (a c) -> p a c", a=4), func=AF.Square),
                                      reads=[(ob, None)], writes=[(sq, b2)])
                                yield
                            ssb = ps[4]

                            def ss_mm(ssb=ssb):
                                last = None
                                for h in range(4):
                                    nc.tensor.matmul(ssb[:, h * 128:(h + 1) * 128], onesb[:], sq[:, 2 * h, :], start=True, stop=False)
                                    last = nc.tensor.matmul(ssb[:, h * 128:(h + 1) * 128], onesb[:], sq[:, 2 * h + 1, :], start=False, stop=True)
                                return last
                            P.add("pe", ss_mm, reads=[(onesb, None), (sq, None)], writes=[(ssb, None)])
                            P.add("act", lambda ssb=ssb: nc.scalar.activation(out=rstdg[:, :, :], in_=ssb[:].rearrange("p (h c) -> p h c", h=4), func=AF.Ln, scale=1.0 / 256, bias=EPS),
                                  reads=[(ssb, None)], writes=[(rstdg, None)])
                            P.add("act", lambda: nc.scalar.activation(out=rstdg[:, :, :], in_=rstdg[:, :, :], func=AF.Exp, scale=-0.5),
                                  reads=[(rstdg, None)], writes=[(rstdg, None)])
                            yield
                            for b2 in range(2):
                                ob = ps[5 + b2]
                                for ec in range(2):
                                    P.add("dve", lambda b2=b2, ob=ob, ec=ec: nc.vector.scalar_tensor_tensor(
                                        out=tmp_o[:, :, :].rearrange("p (h e) c -> p h e c", e=2)[:, b2 * 2:(b2 + 1) * 2, ec, :],
                                        in0=ob[:].rearrange("p (h e c) -> p h e c", h=2, e=2)[:, :, ec, :],
                                        scalar=gfm[:, 48 + ec:49 + ec], in1=rstdg[:, b2 * 2:(b2 + 1) * 2, :], op0=ALU.mult, op1=ALU.mult),
                                        reads=[(ob, None), (gfm, None), (rstdg, None)], writes=[(tmp_o, (b2, ec))])
                            P.add("dve", lambda blk=blk: nc.vector.tensor_tensor(out=o_aT[:, :, blk * 128:(blk + 1) * 128], in0=tmp_o[:, :, :],
                                                                                 in1=silu_g[:, :, blk * 128:(blk + 1) * 128], op=ALU.mult),
                                  reads=[(tmp_o, None), (silu_g, None)], writes=[(o_aT, blk)])
                            yield
                        for b2 in range(2):
                            kb_ = ps[7]

                            def kv_mm(b2=b2, kb_=kb_, blk=blk):
                                last = None
                                for i in range(2):
                                    h = b2 * 2 + i
                                    last = nc.tensor.matmul(kb_[:, i * 256:(i + 1) * 256], k_tok[:, blk, h * 128:(h + 1) * 128],
                                                            gv_tok[:, blk, h * 256:(h + 1) * 256], start=True, stop=True)
                                return last
                            P.add("pe", kv_mm, reads=[(k_tok, blk), (gv_tok, None)], writes=[(kb_, None)])
                            for i in range(2):
                                h = b2 * 2 + i
                                P.add("dve", lambda h=h, i=i, kb_=kb_, blk=blk: nc.vector.scalar_tensor_tensor(
                                    out=Tst[:, h, :], in0=Tst[:, h, :], scalar=decs[:, blk, h:h + 1], in1=kb_[:, i * 256:(i + 1) * 256],
                                    op0=ALU.mult, op1=ALU.add), reads=[(Tst, h), (decs, None), (kb_, None)], writes=[(Tst, h)])
                            yield

                gens = [gla_gen(), dense_gen()]
                if (not own) and tau + 1 < ntiles:
                    gens.append(s0_gen(tau + 1))
                rot["nb"] = 2
                interleave(*gens)
                rot["nb"] = 4
                P.add("dve", lambda: nc.vector.tensor_copy(out=decs[:, 0, :], in_=decs[:, 4, :]), reads=[(decs, None)], writes=[(decs, None)])
                if tau == 1:
                    dump("o_aT1", o_aT, lambda: o_aT[:, 0, :], [128, 512])
                if not own:
                    continue
                if stop_after == "S2":
                    break
                scale = float(128 ** -0.5)
                steps = [(kap, kb) for kap in range(tau, -1, -1) for kb in (3, 2, 1, 0)]
                for hp in range(8 // NH):
                    hs = tuple(NH * hp + i for i in range(NH))
                    psZ = tuple(ps[i % 2] for i in range(NH))
                    psC = tuple(ps[2 + i % 2] for i in range(NH))
                    psO = tuple(ps[4 + i] for i in range(NH))
                    Zd, Cd = dbl[0], dbl[1]
                    for pr in range(2):
                        P.add("dve", lambda pr=pr: nc.vector.memset(SPsp[pr][:, :, :], 0.0), writes=[(SPsp[pr], None)])
                    for hh in range(NH):
                        h = hs[hh]
                        P.add("pe", lambda hh=hh, h=h: nc.tensor.matmul(psO[hh][:], zerosb[:], qT[:, h, :], start=True, stop=False),
                              reads=[(zerosb, None), (qT, h)], writes=[(psO[hh], None)])

                    def kv_load(kap):
                        sl = kap % 2
                        for hh in range(NH):
                            h = hs[hh]
                            P.add("sp", lambda hh=hh, h=h, kap=kap, sl=sl: nc.sync.dma_start(out=Kt[hh][sl][:], in_=kcache[h, :, kap * 512:(kap + 1) * 512]),
                                  reads=[(kcb, kap)], writes=[(Kt[hh][sl], None)], dma=True)
                            P.add("sp", lambda hh=hh, h=h, kap=kap, sl=sl: nc.sync.dma_start(
                                out=Vt[hh][sl][:, :, :], in_=vcache[kap * 4:(kap + 1) * 4, :, h * 128:(h + 1) * 128].rearrange("b t d -> t b d")),
                                reads=[(vcb, kap)], writes=[(Vt[hh][sl], None)], dma=True)

                    def stage_qk(si):
                        kap, kb = steps[si]
                        diag = kap == tau
                        qlo = kb * 128 if diag else 0
                        sl = kap % 2
                        for pr in range(2):
                            for hh in (2 * pr, 2 * pr + 1):
                                h = hs[hh]
                                P.add("pe", lambda hh=hh, h=h, sl=sl, kb=kb, qlo=qlo: nc.tensor.matmul(
                                    psZ[hh][:, qlo:512], Kt[hh][sl][:, kb * 128:(kb + 1) * 128], qT[:, h, qlo:512], start=True, stop=True),
                                    reads=[(Kt[hh][sl], None), (qT, h)], writes=[(psZ[hh], None)])
                            P.add("act", lambda pr=pr, qlo=qlo: nc.scalar.activation(
                                out=Ep[pr][:, :, qlo:512], in_=Zd[:].rearrange("p (h c) -> p h c", h=2)[:, :, qlo:512], func=AF.Exp, scale=scale),
                                reads=[(Zd, None)], writes=[(Ep[pr], None)])
                            if diag:
                                P.add("dve", lambda pr=pr, qlo=qlo: nc.vector.tensor_tensor(
                                    out=Ep[pr][:, :, qlo:qlo + 128], in0=Ep[pr][:, :, qlo:qlo + 128],
                                    in1=maskS().unsqueeze(1).to_broadcast([128, 2, 128]), op=ALU.mult),
                                    reads=[(Ep[pr], None), (cst, None)], writes=[(Ep[pr], None)])

                    kv_load(tau)
                    if tau >= 1:
                        kv_load(tau - 1)
                    stage_qk(0)
                    for si, (kap, kb) in enumerate(steps):
                        first = si == 0
                        last_ = si == len(steps) - 1
                        diag = kap == tau
                        qlo = kb * 128 if diag else 0
                        sl = kap % 2
                        if kb == 3 and kap < tau and kap >= 1:
                            kv_load(kap - 1)
                        for pr in range(2):
                            P.add("act", lambda pr=pr, qlo=qlo: nc.scalar.activation(out=SPp[pr][:, :, qlo:512], in_=Ep[pr][:, :, qlo:512], func=AF.Ln, bias=1.0),
                                  reads=[(Ep[pr], None)], writes=[(SPp[pr], None)])
                        for pr in range(2):
                            for hh in (2 * pr, 2 * pr + 1):
                                def c_mm(hh=hh, pr=pr, qlo=qlo, first=first):
                                    i1 = nc.tensor.matmul(psC[hh][:, qlo:512], negUb[:], SPp[pr][:, hh % 2, qlo:512], start=True, stop=first)
                                    if not first:
                                        i1 = nc.tensor.matmul(psC[hh][:, qlo:512], negonesb[:], SPsp[pr][:, hh % 2, qlo:512], start=False, stop=True)
                                    return i1
                                P.add("pe", c_mm, reads=[(negUb, None), (SPp[pr], None), (SPsp[pr], None), (negonesb, None)], writes=[(psC[hh], None)])
                            P.add("act", lambda pr=pr, qlo=qlo: nc.scalar.activation(
                                out=ECp[pr][:, :, qlo:512], in_=Cd[:].rearrange("p (h c) -> p h c", h=2)[:, :, qlo:512], func=AF.Exp),
                                reads=[(Cd, None)], writes=[(ECp[pr], None)])
                        for pr in range(2):
                            P.add("dve", lambda pr=pr, qlo=qlo: nc.vector.tensor_tensor(out=Ap[pr][:, :, qlo:512], in0=Ep[pr][:, :, qlo:512], in1=ECp[pr][:, :, qlo:512], op=ALU.mult),
                                  reads=[(Ep[pr], None), (ECp[pr], None)], writes=[(Ap[pr], None)])
                            if not last_:
                                P.add("dve", lambda pr=pr, qlo=qlo: nc.vector.tensor_tensor(out=SPsp[pr][:, :, qlo:512], in0=SPsp[pr][:, :, qlo:512], in1=SPp[pr][:, :, qlo:512], op=ALU.add),
                                      reads=[(SPsp[pr], None), (SPp[pr], None)], writes=[(SPsp[pr], None)])
                        if not last_:
                            stage_qk(si + 1)
                        for hh in range(NH):
                            P.add("pe", lambda hh=hh, sl=sl, kb=kb, qlo=qlo, last_=last_: nc.tensor.matmul(
                                psO[hh][:, qlo:512], Vt[hh][sl][:, kb, :], Ap[hh // 2][:, hh % 2, qlo:512], start=False, stop=last_),
                                reads=[(Vt[hh][sl], None), (Ap[hh // 2], None)], writes=[(psO[hh], None)])
                    for hh in range(NH):
                        h = hs[hh]
                        P.add("act", lambda hh=hh, h=h: nc.scalar.copy(out=o_bT[:, h, :], in_=psO[hh][:]), reads=[(psO[hh], None)], writes=[(o_bT, h)])
                if tau == 1:
                    dump("o_bT1", o_bT, lambda: o_bT[:, 0, :], [128, 512])
                if stop_after == "S3":
                    break
                for blk in range(4):
                    P.add("sp", lambda tau=tau, blk=blk: nc.sync.dma_start(out=xh[blk][:], in_=xs[tau, blk]), writes=[(xh[blk], None)], dma=True)
                P.add("sp", lambda: nc.sync.dma_start(out=gbc[:], in_=gtm_d[0]), writes=[(gbc, None)], dma=True)
                K8 = list(range(8))
                for j in range(4):
                    rb = panel(w_in_p[12 + j], 8192)
                    for fc in range(4):
                        bank = dbank()
                        fm_group(rb, fc * 128, K16, lambda kc: uT[:, kc, :], [(uT, None)], bank)
                        P.add("act", lambda fc=fc, bank=bank: nc.scalar.activation(out=sga[fc][:], in_=bank[:], func=AF.Sigmoid),
                              reads=[(bank, None)], writes=[(sga[fc], None)])
                    rb = panel(w_bg[j], 4096)
                    for fc in range(4):
                        bank = dbank()
                        fm_group(rb, fc * 128, K8, lambda kc: o_aT[:, kc, :], [(o_aT, None)], bank)
                        P.add("dve", lambda fc=fc, bank=bank: nc.vector.tensor_tensor(out=sga[fc][:], in0=bank[:], in1=sga[fc][:], op=ALU.mult),
                              reads=[(bank, None), (sga[fc], None)], writes=[(sga[fc], None)])
                    rb = panel(w_in_p[16 + j], 8192)
                    for fc in range(4):
                        bank = dbank()
                        fm_group(rb, fc * 128, K16, lambda kc: uT[:, kc, :], [(uT, None)], bank)
                        P.add("act", lambda fc=fc, bank=bank: nc.scalar.activation(out=sgb[fc][:], in_=bank[:], func=AF.Sigmoid),
                              reads=[(bank, None)], writes=[(sgb[fc], None)])
                    rb = panel(w_bs[j], 4096)
                    for fc in range(4):
                        bank = dbank()
                        fm_group(rb, fc * 128, K8, lambda kc: o_bT[:, kc, :], [(o_bT, None)], bank)
                        P.add("dve", lambda fc=fc, bank=bank: nc.vector.tensor_tensor(out=sgb[fc][:], in0=bank[:], in1=sgb[fc][:], op=ALU.mult),
                              reads=[(bank, None), (sgb[fc], None)], writes=[(sgb[fc], None)])
                        P.add("dve", lambda fc=fc, j=j: nc.vector.tensor_tensor(out=yT[:, 4 * j + fc, :], in0=sga[fc][:], in1=sgb[fc][:], op=ALU.add),
                              reads=[(sga[fc], None), (sgb[fc], None)], writes=[(yT, 4 * j + fc)])
                if stop_after == "S4a":
                    break
                if tau == 1:
                    dump("yT1", yT, lambda: yT[:, 0, :], [128, 512])
                for j in range(4):
                    rb = panel(w_out_p[j], 8192)
                    for blk in range(4):
                        bank = dbank()
                        tm_group(rb, K16, lambda kc, blk=blk: yT[:, kc, blk * 128:(blk + 1) * 128], [(yT, None)], bank)
                        P.add("dve", lambda blk=blk, j=j, bank=bank: nc.vector.tensor_copy(out=m_sb[blk][:, j * 512:(j + 1) * 512], in_=bank[:]),
                              reads=[(bank, None)], writes=[(m_sb[blk], j)])

                def post_norm_add(srcs, gb, gkey):
                    for blk in range(4):
                        c0 = 8 + 3 * (blk % 2)
                        jn = xn[blk % 2]
                        P.add("act", lambda blk=blk, jn=jn, c0=c0: nc.scalar.activation(out=jn[:], in_=srcs[blk][:], func=AF.Square, accum_out=sst[:, c0:c0 + 1]),
                              reads=[(srcs[blk], None)], writes=[(jn, None), (sst, c0)])
                        P.add("act", lambda c0=c0: nc.scalar.activation(out=sst[:, c0 + 1:c0 + 2], in_=sst[:, c0:c0 + 1], func=AF.Ln, scale=1.0 / 2048, bias=EPS),
                              reads=[(sst, c0)], writes=[(sst, c0 + 1)])
                        P.add("act", lambda c0=c0: nc.scalar.activation(out=sst[:, c0 + 2:c0 + 3], in_=sst[:, c0 + 1:c0 + 2], func=AF.Exp, scale=-0.5),
                              reads=[(sst, c0 + 1)], writes=[(sst, c0 + 2)])
                        P.add("dve", lambda blk=blk, c0=c0: nc.vector.scalar_tensor_tensor(out=srcs[blk][:], in0=srcs[blk][:], scalar=sst[:, c0 + 2:c0 + 3], in1=gb[:],
                                                                                  op0=ALU.mult, op1=ALU.mult),
                              reads=[(srcs[blk], None), (sst, c0 + 2), (gb, None)], writes=[(srcs[blk], None)])
                        P.add("pool", lambda blk=blk: nc.gpsimd.tensor_tensor(out=xh[blk][:], in0=xh[blk][:], in1=srcs[blk][:], op=ALU.add),
                              reads=[(xh[blk], None), (srcs[blk], None)], writes=[(xh[blk], None)])
                post_norm_add(m_sb, gbc, 0)
                if tau == 1:
                    dump("h1", xh[0], lambda: xh[0][:, 0:512], [128, 512])
                if stop_after == "S4":
                    break
                for blk in range(4):
                    norm_transpose(xh[blk], lambda blk=blk: xh[blk][:], 16, blk)
                for half in range(2):
                    for pj in range(8):
                        rb = panel(w_up_p[half * 8 + pj], 8192)
                        for fc in range(4):
                            ci = pj * 4 + fc
                            bank = dbank()
                            fm_group(rb, fc * 128, K16, lambda kc: uT[:, kc, :], [(uT, None)], bank)
                            rt = rtmp[ci % 2]
                            P.add("act", lambda bank=bank, rt=rt: nc.scalar.activation(out=rt[:], in_=bank[:], func=AF.Relu),
                                  reads=[(bank, None)], writes=[(rt, None)])
                            P.add("dve", lambda ci=ci, rt=rt: nc.vector.tensor_tensor(out=aT[:, ci, :], in0=rt[:], in1=rt[:], op=ALU.mult),
                                  reads=[(rt, None)], writes=[(aT, ci)])
                    for j in range(4):
                        for s2 in range(2):
                            rb = panel(w_dn_p[j, half * 2 + s2], 8192)
                            for blk in range(4):
                                bank = ps[4 + blk]
                                tm_group(rb, K16, lambda kc, blk=blk, s2=s2: aT[:, s2 * 16 + kc, blk * 128:(blk + 1) * 128], [(aT, None)], bank,
                                         start=(s2 == 0), stop=(s2 == 1))
                        for blk in range(4):
                            bank = ps[4 + blk]
                            if half == 0:
                                P.add("dve", lambda blk=blk, j=j, bank=bank: nc.vector.tensor_copy(out=f_sb[blk][:, j * 512:(j + 1) * 512], in_=bank[:]),
                                      reads=[(bank, None)], writes=[(f_sb[blk], j)])
                            else:
                                P.add("dve", lambda blk=blk, j=j, bank=bank: nc.vector.tensor_tensor(out=f_sb[blk][:, j * 512:(j + 1) * 512], in0=f_sb[blk][:, j * 512:(j + 1) * 512], in1=bank[:], op=ALU.add),
                                      reads=[(bank, None), (f_sb[blk], j)], writes=[(f_sb[blk], j)])
                P.add("sp", lambda: nc.sync.dma_start(out=gbc2[:], in_=gtm_d[1]), writes=[(gbc2, None)], dma=True)
                post_norm_add(f_sb, gbc2, 1)
                if tau == 1:
                    dump("h2", xh[0], lambda: xh[0][:, 0:512], [128, 512])
                if stop_after == "S5":
                    break
                for blk in range(4):
                    norm_transpose(xh[blk], lambda blk=blk: xh[blk][:], 32, blk)
                P.add("sp", lambda oj=oj: nc.sync.dma_start(out=p_f[:, :, :], in_=pp[oj].rearrange("b t c -> t b c")), writes=[(p_f, None)], dma=True)
                P.add("dve", lambda: nc.vector.tensor_copy(out=p_b[:, :, :], in_=p_f[:, :, :]), reads=[(p_f, None)], writes=[(p_b, None)])
                for blk in range(4):
                    tb = ps[0]

                    def ptr(blk=blk, tb=tb):
                        last = None
                        for kc in range(2):
                            last = nc.tensor.transpose(out=tb[:].bitcast(BF16)[:, kc * 128:(kc + 1) * 128], in_=p_b[:, blk, kc * 128:(kc + 1) * 128], identity=idb[:])
                        return last
                    P.add("pe", ptr, reads=[(p_b, None), (idb, None)], writes=[(tb, None)])
                    P.add("dve", lambda blk=blk, tb=tb: nc.vector.tensor_copy(out=pT[:, :, blk * 128:(blk + 1) * 128], in_=tb[:].bitcast(BF16)[:, 0:256].rearrange("p (a b) -> p a b", a=2)),
                          reads=[(tb, None)], writes=[(pT, blk)])
                def s6_gen(oj=oj):
                    for j in range(4):
                        rbg = panel(w_pg_p[j], 8192)
                        for blk in range(4):
                            gbk = ps[blk % 2]
                            tm_group(rbg, K16, lambda kc, blk=blk: uT[:, kc, blk * 128:(blk + 1) * 128], [(uT, None)], gbk)
                            P.add("act", lambda gbk=gbk, blk=blk: nc.scalar.activation(out=sgp[blk][:], in_=gbk[:], func=AF.Sigmoid), reads=[(gbk, None)], writes=[(sgp[blk], None)])
                            yield
                        rbp = panel(w_pp_p[j], 1024)
                        for blk in range(4):
                            ebk = ps[4 + blk % 2]
                            tm_group(rbp, [0, 1], lambda kc, blk=blk: pT[:, kc, blk * 128:(blk + 1) * 128], [(pT, None)], ebk)
                            P.add("dve", lambda ebk=ebk, blk=blk: nc.vector.tensor_tensor(out=sgp[blk][:], in0=ebk[:], in1=sgp[blk][:], op=ALU.mult),
                                  reads=[(ebk, None), (sgp[blk], None)], writes=[(sgp[blk], None)])
                            P.add("pool", lambda blk=blk, j=j: nc.gpsimd.tensor_tensor(out=xh[blk][:, j * 512:(j + 1) * 512], in0=xh[blk][:, j * 512:(j + 1) * 512], in1=sgp[blk][:], op=ALU.add),
                                  reads=[(xh[blk], None), (sgp[blk], None)], writes=[(xh[blk], None)])
                            yield
                    for blk in range(4):
                        P.add("sp", lambda oj=oj, blk=blk: nc.sync.dma_start(out=y[oj, blk], in_=xh[blk][:]), reads=[(xh[blk], None)], dma=True)

                gens = [s6_gen()]
                if tau + 1 < ntiles:
                    gens.append(s0_gen(tau + 1))
                interleave(*gens)

        sstm = sb("sstm", [128, 16], F32)
        P0 = Prog(nc, spaces, plan=True)
        run(P0)
        P1 = Prog(nc, spaces, plan=False)
        P1.panel_list = P0.panel_log
        run(P1)
        nw = P1.emit(es)
        DEBUG["nwaits"] = nw
        DEBUG["nops"] = P1.nops
        DEBUG["npanels"] = len(P0.panel_log)
    return nc, dbg_out


def _panelize(W):
    K, N = W.shape
    return np.ascontiguousarray(W.reshape(K // 128, 128, N // 512, 512).transpose(2, 1, 0, 3).reshape(N // 512, 128, (K // 128) * 512))


def prep_shared(inp):
    f = lambda a: np.asarray(a, dtype=np.float32)
    w_in = f(inp["w_in"])[0]
    cols = {"gq": (0, 512), "gk": (512, 512), "gv": (1024, 1024), "glr": (2048, 16), "gout": (2064, 1024), "sq": (3088, 1024),
            "sk": (4112, 1024), "sv": (5136, 1024), "ga": (6160, 2048), "gb": (8208, 2048)}
    order = ["gq", "gk", "gv", "gout", "sq", "sk", "sv", "ga", "gb"]
    w_in_p = np.concatenate([_panelize(w_in[:, cols[k][0]:cols[k][0] + cols[k][1]]) for k in order], 0)
    assert w_in_p.shape == (20, 128, 8192)
    glr = w_in[:, 2048:2064]
    w_glr = np.ascontiguousarray(glr.reshape(16, 128, 16).transpose(1, 0, 2).reshape(128, 256))
    wg1 = np.zeros((32, 512), np.float32)
    wg1[0:16] = f(inp["w_gate_up"])[0]
    wg1[16] = f(inp["b_gate"])[0]
    w_dn = f(inp["w_mlp_down"])[0]
    w_dn_p = np.ascontiguousarray(w_dn.reshape(4, 16, 128, 4, 512).transpose(3, 0, 2, 1, 4).reshape(4, 4, 128, 8192))
    gfm = np.zeros((128, 64), np.float32)
    gfm[:, 0:16] = f(inp["norm_mix_pre"])[0].reshape(16, 128).T
    gfm[:, 16:32] = f(inp["norm_mlp_pre"])[0].reshape(16, 128).T
    gfm[:, 32:48] = f(inp["norm_ple"])[0].reshape(16, 128).T
    gfm[:, 48:50] = f(inp["gla_norm"])[0].reshape(2, 128).T
    gtm = np.stack([np.tile(f(inp["norm_mix_post"])[0][None, :], (128, 1)), np.tile(f(inp["norm_mlp_post"])[0][None, :], (128, 1))], 0)
    i = np.arange(128)
    consts = np.zeros((128, 640), np.float32)
    consts[:, 0:128] = np.eye(128)
    consts[:, 128:256] = (i[:, None] < i[None, :])
    consts[:, 256:384] = -1.0 * (i[:, None] >= i[None, :])
    consts[:, 384:512] = (-1.0 / 16.0) * (i[:, None] <= i[None, :])
    consts[:, 512:640] = (i[:, None] <= i[None, :])
    return {
        "w_in_p": w_in_p, "w_glr": w_glr, "wg1": wg1,
        "w_bg": _panelize(f(inp["w_branch_gla"])[0]), "w_bs": _panelize(f(inp["w_branch_sb"])[0]),
        "w_out_p": _panelize(f(inp["w_out"])[0]), "w_up_p": _panelize(f(inp["w_mlp_up"])[0]), "w_dn_p": w_dn_p,
        "w_pg_p": _panelize(f(inp["w_ple_gate"])[0]), "w_pp_p": _panelize(f(inp["w_ple_proj"])[0]),
        "gfm": gfm, "gtm": np.ascontiguousarray(gtm), "consts": consts,
    }


def prep_core(inp, b, c):
    x = np.asarray(inp["x"], dtype=np.float32)
    p = np.asarray(inp["p"], dtype=np.float32)
    xs = np.zeros((NT, 512, 2048), np.float32)
    pp = np.zeros((4, 512, 256), np.float32)
    for j in range(4):
        if c == 1:
            xs[2 * j] = x[b, (2 * j) * 512:(2 * j + 1) * 512]
        elif j >= 1:
            xs[2 * j] = x[b, (2 * j - 1) * 512:(2 * j) * 512]
        g = 2 * j + c
        xs[2 * j + 1] = x[b, g * 512:(g + 1) * 512]
        pp[j] = p[0, b, g * 512:(g + 1) * 512]
    return {"xs": xs.reshape(NT, 4, 128, 2048), "pp": pp.reshape(4, 4, 128, 256)}


_CACHE = {}


def kernel(**inputs):
    if "nc" not in _CACHE:
        _CACHE["nc"] = build()[0]
    nc = _CACHE["nc"]
    shared = prep_shared(inputs)
    in_maps = []
    for core in range(8):
        b, c = core // 2, core % 2
        m = dict(shared)
        m.update(prep_core(inputs, b, c))
        in_maps.append(m)
    res = run_bass_kernel_spmd(nc, in_maps, core_ids=list(range(8)))
    out = np.zeros((4, 4096, 2048), np.float32)
    for core in range(8):
        b, c = core // 2, core % 2
        yy = np.asarray(res.results[core]["y"]).reshape(4, 512, 2048)
        for j in range(4):
            g = 2 * j + c
            out[b, g * 512:(g + 1) * 512] = yy[j]
    return out
```

```python
import numpy as np
import concourse.bass as bass
import concourse.mybir as mybir
from concourse.bass_utils import run_bass_kernel_spmd
from contextlib import ExitStack

F32 = mybir.dt.float32
BF16 = mybir.dt.bfloat16
AF = mybir.ActivationFunctionType
ALU = mybir.AluOpType
AX = mybir.AxisListType

class Buf:
    def __init__(self, name, t, space=None, lo=0, hi=1):
        self.name = name
        self.t = t
        self.space = space if space is not None else name
        self.lo = lo
        self.hi = hi
        self.entries = {}

    def __getitem__(self, k):
        return self.t[k]


class Op:
    __slots__ = ("eng", "fn", "deps", "dma", "idx", "lidx", "sem", "val", "milestone", "mcount", "waits")

    def __init__(self, eng, fn, dma):
        self.eng = eng
        self.fn = fn
        self.deps = {}
        self.dma = dma
        self.sem = None
        self.val = 0
        self.milestone = False
        self.mcount = 0
        self.waits = None


class Prog:
    ENGS = ("pe", "act", "dve", "pool", "sp")
    NDMA = 12

    def __init__(self, nc, spaces=None, plan=False):
        self.nc = nc
        self.plan = plan
        self.ops = {e: [] for e in self.ENGS}
        self.nops = 0
        self.spaces = spaces if spaces is not None else {}
        self.panel_list = []
        self.dma_count = {"sp": 0, "pool": 0, "act": 0}
        self.dma_ops = {"sp": [], "pool": [], "act": []}
        self.panel_log = []

    def buf(self, name, t, space=None, lo=0, hi=1):
        b = Buf(name, t, space, lo, hi)
        self.spaces.setdefault(b.space, []).append(b)
        return b

    def _conf_entries(self, b, k):
        for k2, e in b.entries.items():
            if k is None or k2 is None or k2 == k:
                yield e
        sp = self.spaces[b.space]
        if len(sp) > 1:
            for b2 in sp:
                if b2 is not b and b2.lo < b.hi and b.lo < b2.hi:
                    for e in b2.entries.values():
                        yield e

    def add(self, eng, fn, reads=(), writes=(), dma=False):
        if self.plan:
            return None
        op = Op(eng, fn, dma)
        op.idx = self.nops
        self.nops += 1
        op.lidx = len(self.ops[eng])
        deps = op.deps
        for (b, k) in reads:
            for e in self._conf_entries(b, k):
                if e[0] is not None:
                    deps[e[0]] = True
        for (b, k) in writes:
            for e in self._conf_entries(b, k):
                if e[0] is not None and e[0] not in deps:
                    deps[e[0]] = False
                for r in e[1].values():
                    if r not in deps:
                        deps[r] = False
                for r in e[2]:
                    if r not in deps:
                        deps[r] = False
        for (b, k) in reads:
            e = b.entries.get(k)
            if e is None:
                e = b.entries[k] = [None, {}, []]
            if dma:
                e[2].append(op)
            else:
                e[1][eng] = op
        for (b, k) in writes:
            if k is None:
                for kk in list(b.entries.keys()):
                    if kk is not None:
                        del b.entries[kk]
            b.entries[k] = [op, {}, []]
        deps.pop(op, None)
        if dma:
            q = eng
            n = self.dma_count[q]
            self.dma_count[q] += 1
            op.sem = (q, n % self.NDMA)
            op.val = 16 * (n // self.NDMA + 1)
            if n >= self.NDMA:
                deps[self.dma_ops[q][n - self.NDMA]] = True
            self.dma_ops[q].append(op)
        self.ops[eng].append(op)
        return op

    def emit(self, es):
        nc = self.nc
        engobj = {"pe": nc.tensor, "act": nc.scalar, "dve": nc.vector, "pool": nc.gpsimd, "sp": nc.sync}
        esem = {e: es.enter_context(nc.semaphore("sem_" + e)) for e in ("pe", "act", "dve", "pool")}
        dsem = {}
        for q in ("sp", "pool", "act"):
            if self.dma_count[q]:
                for i in range(min(self.NDMA, self.dma_count[q])):
                    dsem[(q, i)] = es.enter_context(nc.semaphore("dsem_%s_%d" % (q, i)))
        for E in self.ENGS:
            seen = {}
            seend = {}
            for op in self.ops[E]:
                w = []
                for p, raw in op.deps.items():
                    if p.dma:
                        if seend.get(p.sem, 0) >= p.val:
                            continue
                        seend[p.sem] = p.val
                        w.append(p)
                    else:
                        if p.eng == E:
                            if E == "pe" or not raw:
                                continue
                        if seen.get(p.eng, -1) >= p.lidx:
                            continue
                        seen[p.eng] = p.lidx
                        p.milestone = True
                        w.append(p)
                op.waits = w
        for E in self.ENGS:
            c = 0
            for op in self.ops[E]:
                if op.milestone and not op.dma:
                    c += 1
                op.mcount = c
        nwait = 0
        for E in self.ENGS:
            eo = engobj[E]
            for op in self.ops[E]:
                best = {}
                for p in op.waits:
                    if p.dma:
                        key = dsem[p.sem]
                        v = p.val
                    else:
                        key = esem[p.eng]
                        v = p.mcount
                    if best.get(key, (0, None))[0] < v:
                        best[key] = (v, key)
                for v, key in best.values():
                    eo.wait_ge(key, v)
                    nwait += 1
                inst = op.fn()
                if op.dma:
                    inst.then_inc(dsem[op.sem], 16)
                elif op.milestone:
                    inst.then_inc(esem[E], 1)
        for q in ("sp", "pool", "act"):
            n = self.dma_count[q]
            for i in range(min(self.NDMA, n)):
                cnt = (n - 1 - i) // self.NDMA + 1
                nc.sync.wait_ge(dsem[(q, i)], 16 * cnt)
        return nwait

NT = 8
TT = 512
EPS = 1e-6
W_IN_ORDER_OTHER = [1, 2, 3, 8, 9, 10, 11]
DEBUG = {}


def build(ntiles=NT, stop_after=None, dbg=()):
    nc = bass.Bass("TRN2", target_bir_lowering=False)

    def dram(name, shape, dt=F32, kind="ExternalInput"):
        return nc.dram_tensor(name, shape, dt, kind=kind).ap()

    xs = dram("xs", [NT, 4, 128, 2048])
    pp = dram("pp", [4, 4, 128, 256])
    w_in_p = dram("w_in_p", [20, 128, 8192])
    w_glr = dram("w_glr", [128, 256])
    wg1 = dram("wg1", [32, 512])
    w_bg = dram("w_bg", [4, 128, 4096])
    w_bs = dram("w_bs", [4, 128, 4096])
    w_out_p = dram("w_out_p", [4, 128, 8192])
    w_up_p = dram("w_up_p", [16, 128, 8192])
    w_dn_p = dram("w_dn_p", [4, 4, 128, 8192])
    w_pg_p = dram("w_pg_p", [4, 128, 8192])
    w_pp_p = dram("w_pp_p", [4, 128, 1024])
    gfm_d = dram("gfm", [128, 64])
    gtm_d = dram("gtm", [2, 128, 2048])
    consts_d = dram("consts", [128, 640])
    y = dram("y", [4, 4, 128, 2048], F32, "ExternalOutput")
    kcache = dram("kcache", [8, 128, NT * TT], BF16, "Internal")
    vcache = dram("vcache", [NT * 4, 128, 1024], BF16, "Internal")
    dbg_out = {}

    with ExitStack() as es:
        e = es.enter_context
        spaces = {}

        def sb(name, shape, dt):
            t = e(nc.sbuf_tensor(name, shape, dt))
            b = Buf(name, t)
            spaces.setdefault(b.space, []).append(b)
            return b

        ring = [sb("ring%d" % i, [128, 8192], BF16) for i in range(3)]
        xh = [sb("xh%d" % i, [128, 2048], F32) for i in range(4)]
        xin = [sb("xin%d" % i, [128, 2048], F32) for i in range(2)]
        xn = [sb("xn%d" % i, [128, 2048], BF16) for i in range(2)]
        uT = sb("uT", [128, 16, 512], BF16)
        Tst = sb("Tst", [128, 4, 256], F32)
        S_bf = sb("S_bf", [128, 4, 256], BF16)
        decs = sb("decs", [128, 5, 4], F32)
        gfm = sb("gfm_sb", [128, 64], F32)
        cst = sb("cst", [128, 640], F32)
        idb = sb("idb", [128, 128], BF16)
        onesb = sb("onesb", [128, 128], BF16)
        zerosb = sb("zerosb", [128, 128], BF16)
        negonesb = sb("negonesb", [128, 128], BF16)
        negUb = sb("negUb", [128, 128], BF16)
        wglr = sb("wglr", [128, 256], BF16)
        wg1sb = sb("wg1sb", [32, 512], F32)
        glrT1 = sb("glrT1", [32, 512], F32)
        sst = sb("sst", [128, 16], F32)
        Ut = e(nc.sbuf_tensor("U", [128, 36864], BF16))
        ps = []
        for i in range(8):
            t = e(nc.psum_tensor("ps%d" % i, [128, 512], F32))
            b = Buf("ps%d" % i, t)
            spaces.setdefault(b.space, []).append(b)
            ps.append(b)
        kcb = Buf("kcache", None)
        vcb = Buf("vcache", None)
        spaces["kcache"] = [kcb]
        spaces["vcache"] = [vcb]

        def carve(name, off_kb, shape, dt):
            nel = int(np.prod(shape[1:]))
            nbytes = nel * (4 if dt == F32 else 2)
            lo = int(off_kb * 1024)
            assert lo + nbytes <= 36864 * 2, name
            v = Ut[:, lo // 2: (lo + nbytes) // 2]
            if dt == F32:
                v = v.bitcast(F32)
            if len(shape) == 3:
                v = v.rearrange("p (a b) -> p a b", a=shape[1])
            elif len(shape) == 4:
                v = v.rearrange("p (a b c) -> p a b c", a=shape[1], b=shape[2])
            b = Buf(name, v, "U", lo, lo + nbytes)
            spaces.setdefault("U", []).append(b)
            return b

        sp_tok = carve("sp_tok", 0, [128, 4, 512], F32)
        eb = carve("eb", 8, [128, 4, 512], F32)
        enb = carve("enb", 16, [128, 4, 512], F32)
        kiT = carve("kiT", 24, [128, 4, 512], BF16)
        qdT = carve("qdT", 52, [128, 4, 512], BF16)
        gv_tok = carve("gv_tok", 28, [128, 4, 1024], BF16)
        kst = carve("kst", 36, [128, 8, 512], BF16)
        vst = carve("vst", 44, [128, 4, 1024], BF16)
        silu_g = carve("silu_g", 56, [128, 8, 512], BF16)
        qT = carve("qT", 64, [128, 8, 512], BF16)
        uT_alt = carve("uT_alt", 56, [128, 16, 512], BF16)
        o_aT = carve("o_aT", 0, [128, 8, 512], BF16)
        k_tok = carve("k_tok", 8, [128, 4, 512], BF16)
        scm = carve("scm", 12, [128, 4, 128], BF16)
        sq = carve("sq", 13, [128, 8, 128], BF16)
        rstdg = carve("rstdg", 16, [128, 4, 128], F32)
        tmp_o = carve("tmp_o", 18, [128, 8, 128], F32)
        o_bT = carve("o_bT", 56, [128, 8, 512], BF16)
        NH = 4
        Eb = [carve("E%d" % i, 8 + 11 * i + 0, [128, 512], F32) for i in range(NH)]
        SPb = [carve("SP%d" % i, 8 + 11 * i + 2, [128, 512], BF16) for i in range(NH)]
        SPs = [carve("SPs%d" % i, 8 + 11 * i + 3, [128, 512], BF16) for i in range(NH)]
        ECb = [carve("EC%d" % i, 8 + 11 * i + 4, [128, 512], F32) for i in range(NH)]
        Ab = [carve("A%d" % i, 8 + 11 * i + 6, [128, 512], BF16) for i in range(NH)]
        Kt = [[carve("Kt%d%d" % (i, j), 8 + 11 * i + 7 + j, [128, 512], BF16) for j in range(2)] for i in range(NH)]
        Vt = [[carve("Vt%d%d" % (i, j), 8 + 11 * i + 9 + j, [128, 4, 128], BF16) for j in range(2)] for i in range(NH)]
        yT = carve("yT", 8, [128, 16, 512], BF16)
        sga = [carve("sga%d" % i, 24 + 2 * i, [128, 512], F32) for i in range(4)]
        sgb = [carve("sgb%d" % i, 32 + 2 * i, [128, 512], F32) for i in range(4)]
        m_sb = [carve("m_sb0", 40, [128, 2048], F32), carve("m_sb1", 48, [128, 2048], F32),
                carve("m_sb2", 0, [128, 2048], F32), carve("m_sb3", 56, [128, 2048], F32)]
        gbc = carve("gbc", 64, [128, 2048], F32)
        aT = carve("aT", 0, [128, 32, 512], BF16)
        f_sb = [carve("f_sb%d" % i, 32 + 8 * i, [128, 2048], F32) for i in range(4)]
        rtmp = [carve("rtmp%d" % i, 64 + 2 * i, [128, 512], F32) for i in range(2)]
        gbc2 = carve("gbc2", 0, [128, 2048], F32)
        p_f = carve("p_f", 0, [128, 4, 256], F32)
        p_b = carve("p_b", 4, [128, 4, 256], BF16)
        pT = carve("pT", 6, [128, 2, 512], BF16)
        sgp = [carve("sgp%d" % i, 8 + 2 * i, [128, 512], F32) for i in range(4)]

        def run(P):
            plan = P.plan
            ring_state = {"n": 0, "issued": 0}

            def panel(ap, nel):
                i = ring_state["n"]
                ring_state["n"] += 1
                if plan:
                    P.panel_log.append((ap, nel))
                    return ring[i % 3]
                while ring_state["issued"] < min(i + 3, len(P.panel_list)):
                    j = ring_state["issued"]
                    apj, nelj = P.panel_list[j]
                    rb = ring[j % 3]
                    P.add("pool", lambda apj=apj, nelj=nelj, rb=rb: nc.gpsimd.dma_start(out=rb[:, 0:nelj], in_=apj),
                          writes=[(rb, None)], dma=True)
                    ring_state["issued"] += 1
                return ring[i % 3]

            rot = {"d": 0, "ev": 0, "nb": 4}

            def dbank():
                rot["d"] = (rot["d"] + 1) % rot["nb"]
                return ps[rot["d"]]

            def interleave(*gens):
                alive = list(gens)
                while alive:
                    for g in list(alive):
                        try:
                            next(g)
                        except StopIteration:
                            alive.remove(g)

            def evac(out_fn, in_fn, reads, writes, eng=None):
                if eng is None:
                    rot["ev"] ^= 1
                    eng = "act" if rot["ev"] else "dve"
                if eng == "act":
                    P.add("act", lambda: nc.scalar.copy(out=out_fn(), in_=in_fn()), reads=reads, writes=writes)
                else:
                    P.add("dve", lambda: nc.vector.tensor_copy(out=out_fn(), in_=in_fn()), reads=reads, writes=writes)

            def dump(name, buf, apfn, shape):
                if name in dbg and not plan:
                    if name not in dbg_out:
                        dbg_out[name] = dram("dbg_" + name, shape, F32, "ExternalOutput")
                    stage = e(nc.sbuf_tensor("dbgs_" + name, shape, F32))
                    sbf = Buf("dbgs_" + name, stage)
                    spaces[sbf.space] = [sbf]
                    P.add("dve", lambda: nc.vector.tensor_copy(out=stage[:], in_=apfn()), reads=[(buf, None)], writes=[(sbf, None)])
                    P.add("sp", lambda: nc.sync.dma_start(out=dbg_out[name], in_=stage[:]), reads=[(sbf, None)], dma=True)

            P.add("sp", lambda: nc.sync.dma_start(out=gfm[:], in_=gfm_d), writes=[(gfm, None)], dma=True)
            P.add("sp", lambda: nc.sync.dma_start(out=cst[:], in_=consts_d), writes=[(cst, None)], dma=True)
            P.add("sp", lambda: nc.sync.dma_start(out=wg1sb[:], in_=wg1), writes=[(wg1sb, None)], dma=True)
            P.add("pool", lambda: nc.gpsimd.dma_start(out=wglr[:], in_=w_glr), writes=[(wglr, None)], dma=True)
            P.add("dve", lambda: nc.vector.tensor_copy(out=idb[:], in_=cst[:, 0:128]), reads=[(cst, None)], writes=[(idb, None)])
            P.add("dve", lambda: nc.vector.memset(onesb[:], 1.0), writes=[(onesb, None)])
            P.add("dve", lambda: nc.vector.memset(zerosb[:], 0.0), writes=[(zerosb, None)])
            P.add("dve", lambda: nc.vector.memset(negonesb[:], -1.0), writes=[(negonesb, None)])
            P.add("dve", lambda: nc.vector.tensor_copy(out=negUb[:], in_=cst[:, 256:384]), reads=[(cst, None)], writes=[(negUb, None)])
            P.add("dve", lambda: nc.vector.memset(glrT1[:], 1.0), writes=[(glrT1, None)])
            P.add("dve", lambda: nc.vector.memset(Tst[:], 0.0), writes=[(Tst, None)])
            P.add("dve", lambda: nc.vector.memset(decs[:], 1.0), writes=[(decs, None)])
            maskS = lambda: cst[:, 128:256]
            negU = lambda: cst[:, 256:384]
            triN = lambda: cst[:, 384:512]
            maskC = lambda: cst[:, 512:640]

            def norm_transpose_g(src_buf, src_fn, gcol, blk, dst=None, banks=None):
                dst = uT if dst is None else dst
                xnb = xn[blk % 2]
                c0 = 3 * (blk % 2)
                P.add("act", lambda: nc.scalar.activation(out=xnb[:], in_=src_fn(), func=AF.Square, accum_out=sst[:, c0:c0 + 1]),
                      reads=[(src_buf, None)], writes=[(xnb, None), (sst, c0)])
                P.add("act", lambda: nc.scalar.activation(out=sst[:, c0 + 1:c0 + 2], in_=sst[:, c0:c0 + 1], func=AF.Ln, scale=1.0 / 2048, bias=EPS),
                      reads=[(sst, c0)], writes=[(sst, c0 + 1)])
                P.add("act", lambda: nc.scalar.activation(out=sst[:, c0 + 2:c0 + 3], in_=sst[:, c0 + 1:c0 + 2], func=AF.Exp, scale=-0.5),
                      reads=[(sst, c0 + 1)], writes=[(sst, c0 + 2)])
                P.add("act", lambda: nc.scalar.activation(out=xnb[:], in_=src_fn(), func=AF.Copy, scale=sst[:, c0 + 2:c0 + 3]),
                      reads=[(src_buf, None), (sst, c0 + 2)], writes=[(xnb, None)])
                yield
                for half in range(2):
                    bank = ps[2 * (blk % 2) + half] if banks is None else banks[half]

                    def tr(half=half, bank=bank):
                        last = None
                        for k in range(8):
                            kc = half * 8 + k
                            last = nc.tensor.transpose(out=bank[:].bitcast(BF16)[:, k * 128:(k + 1) * 128],
                                                       in_=xnb[:, kc * 128:(kc + 1) * 128], identity=idb[:])
                        return last
                    P.add("pe", tr, reads=[(xnb, None), (idb, None)], writes=[(bank, None)])
                    P.add("dve", lambda half=half, bank=bank: nc.vector.tensor_tensor(
                        out=dst[:, half * 8:(half + 1) * 8, blk * 128:(blk + 1) * 128],
                        in0=bank[:].bitcast(BF16).rearrange("p (a b) -> p a b", a=8),
                        in1=gfm[:, gcol + half * 8: gcol + half * 8 + 8].unsqueeze(2).to_broadcast([128, 8, 128]),
                        op=ALU.mult), reads=[(bank, None), (gfm, None)], writes=[(dst, blk)])
                    yield

            def norm_transpose(src_buf, src_fn, gcol, blk):
                for _ in norm_transpose_g(src_buf, src_fn, gcol, blk):
                    pass

            def s0_gen(tau_):
                dst = uT if tau_ % 2 == 1 else uT_alt
                for blk in range(4):
                    xb_ = xin[blk % 2]
                    P.add("sp", lambda tau_=tau_, blk=blk, xb_=xb_: nc.sync.dma_start(out=xb_[:], in_=xs[tau_, blk]),
                          writes=[(xb_, None)], dma=True)
                    yield
                    yield from norm_transpose_g(xb_, lambda xb_=xb_: xb_[:], 0, blk, dst=dst, banks=(ps[2], ps[3]))

            def fm_group(rb, ncol0, kcs, rhs_fn, rhs_reads, bank, ncols=512, m=128, out_fn=None):
                def f():
                    last = None
                    o = out_fn() if out_fn else bank[:, 0:ncols]
                    for i, kc in enumerate(kcs):
                        last = nc.tensor.matmul(o, rb[:, kc * 512 + ncol0: kc * 512 + ncol0 + m], rhs_fn(kc),
                                                start=(i == 0), stop=(i == len(kcs) - 1))
                    return last
                P.add("pe", f, reads=[(rb, None)] + rhs_reads, writes=[(bank, None)])

            def tm_group(rb, kcs, lhs_fn, lhs_reads, bank, start=True, stop=True):
                def f():
                    last = None
                    for i, kc in enumerate(kcs):
                        last = nc.tensor.matmul(bank[:], lhs_fn(kc), rb[:, kc * 512:(kc + 1) * 512],
                                                start=(start and i == 0), stop=(stop and i == len(kcs) - 1))
                    return last
                P.add("pe", f, reads=[(rb, None)] + lhs_reads, writes=[(bank, None)])

            K16 = list(range(16))

            for _ in s0_gen(0):
                pass
            for tau in range(ntiles):
                own = (tau % 2 == 1)
                oj = tau // 2
                UT = uT if own else uT_alt
                def glr_mm(UT=UT):
                    last = None
                    for kc in range(16):
                        last = nc.tensor.matmul(ps[2][0:16, :], wglr[:, kc * 16:(kc + 1) * 16], UT[:, kc, :],
                                                start=(kc == 0), stop=(kc == 15))
                    return last
                P.add("pe", glr_mm, reads=[(wglr, None), (UT, None)], writes=[(ps[2], None)])
                P.add("act", lambda: nc.scalar.copy(out=glrT1[0:16, :], in_=ps[2][0:16, :]), reads=[(ps[2], None)], writes=[(glrT1, None)])
                def gv_half(half, UT=UT):
                    rb = panel(w_in_p[2 + half], 8192)
                    for blk in range(4):
                        bank = dbank()
                        tm_group(rb, K16, lambda kc, blk=blk, UT=UT: UT[:, kc, blk * 128:(blk + 1) * 128], [(UT, None)], bank)
                        evac(lambda blk=blk, half=half: gv_tok[:, blk, half * 512:(half + 1) * 512], lambda bank=bank: bank[:],
                             [(bank, None)], [(gv_tok, (blk, half))])
                rot["nb"] = 2
                gv_half(0)
                for blk in range(4):
                    bank = ps[3 if blk % 2 == 0 else 2]
                    P.add("pe", lambda blk=blk, bank=bank: nc.tensor.matmul(bank[:], glrT1[0:32, blk * 128:(blk + 1) * 128], wg1sb[0:32, :], start=True, stop=True),
                          reads=[(glrT1, None), (wg1sb, None)], writes=[(bank, None)])
                    P.add("act", lambda blk=blk, bank=bank: nc.scalar.activation(out=sp_tok[:, blk, :], in_=bank[:], func=AF.Exp, scale=-1.0),
                          reads=[(bank, None)], writes=[(sp_tok, blk)])
                    P.add("act", lambda blk=blk: nc.scalar.activation(out=sp_tok[:, blk, :], in_=sp_tok[:, blk, :], func=AF.Ln, bias=1.0),
                          reads=[(sp_tok, blk)], writes=[(sp_tok, blk)])
                gv_half(1)
                rot["nb"] = 4
                for h in range(4):
                    bank = ps[4 + h]

                    def bt_mm(h=h, bank=bank):
                        last = None
                        for blk in range(4):
                            last = nc.tensor.matmul(bank[:, blk * 128:(blk + 1) * 128], sp_tok[:, blk, h * 128:(h + 1) * 128], triN(), start=True, stop=True)
                        return last
                    P.add("pe", bt_mm, reads=[(sp_tok, None), (cst, None)], writes=[(bank, None)])
                    P.add("act", lambda h=h, bank=bank: nc.scalar.activation(out=eb[:, h, :], in_=bank[:], func=AF.Exp),
                          reads=[(bank, None)], writes=[(eb, h)])
                    P.add("act", lambda h=h, bank=bank: nc.scalar.activation(out=enb[:, h, :], in_=bank[:], func=AF.Exp, scale=-1.0),
                          reads=[(bank, None)], writes=[(enb, h)])
                    P.add("dve", lambda h=h: nc.vector.tensor_copy(out=decs[:, 1:5, h], in_=eb[:, h, 127:512:128]),
                          reads=[(eb, h)], writes=[(decs, None)])
                if tau == 1:
                    dump("eb1", eb, lambda: eb[:, 0, :], [128, 512])
                rb = panel(w_in_p[1], 8192)
                for h in range(4):
                    bank = dbank()
                    fm_group(rb, h * 128, K16, lambda kc, UT=UT: UT[:, kc, :], [(UT, None)], bank)
                    P.add("dve", lambda h=h, bank=bank: nc.vector.tensor_tensor(out=kiT[:, h, :], in0=bank[:], in1=enb[:, h, :], op=ALU.mult),
                          reads=[(bank, None), (enb, h)], writes=[(kiT, h)])
                if own:
                    rb = panel(w_in_p[0], 8192)
                    for h in range(4):
                        bank = dbank()
                        fm_group(rb, h * 128, K16, lambda kc, UT=UT: UT[:, kc, :], [(UT, None)], bank)
                        P.add("dve", lambda h=h, bank=bank: nc.vector.scalar_tensor_tensor(
                            out=qdT[:, h, :], in0=bank[:], scalar=float(128 ** -0.5), in1=eb[:, h, :], op0=ALU.mult, op1=ALU.mult),
                            reads=[(bank, None), (eb, h)], writes=[(qdT, h)])
                if own:
                    for half in range(2):
                        rb = panel(w_in_p[4 + half], 8192)
                        for cc in range(4):
                            c8 = half * 4 + cc
                            bank = dbank()
                            fm_group(rb, cc * 128, K16, lambda kc, UT=UT: UT[:, kc, :], [(UT, None)], bank)
                            P.add("act", lambda c8=c8, bank=bank: nc.scalar.activation(out=silu_g[:, c8, :], in_=bank[:], func=AF.Silu),
                                  reads=[(bank, None)], writes=[(silu_g, c8)])

                def dense_gen(tau=tau, own=own):
                    for half in range(2):
                        rb = panel(w_in_p[8 + half], 8192)
                        for hh in range(4):
                            h = half * 4 + hh
                            bank = dbank()
                            fm_group(rb, hh * 128, K16, lambda kc, UT=UT: UT[:, kc, :], [(UT, None)], bank)
                            evac(lambda h=h: kst[:, h, :], lambda bank=bank: bank[:], [(bank, None)], [(kst, h)])
                            yield
                    P.add("sp", lambda tau=tau: nc.sync.dma_start(out=kcache[:, :, tau * 512:(tau + 1) * 512].rearrange("h d t -> d h t"), in_=kst[:, :, :]),
                          reads=[(kst, None)], writes=[(kcb, tau)], dma=True)
                    for half in range(2):
                        rb = panel(w_in_p[10 + half], 8192)
                        for blk in range(4):
                            bank = dbank()
                            tm_group(rb, K16, lambda kc, blk=blk, UT=UT: UT[:, kc, blk * 128:(blk + 1) * 128], [(UT, None)], bank)
                            evac(lambda blk=blk, half=half: vst[:, blk, half * 512:(half + 1) * 512], lambda bank=bank: bank[:],
                                 [(bank, None)], [(vst, (blk, half))])
                            yield
                    P.add("sp", lambda tau=tau: nc.sync.dma_start(out=vcache[tau * 4:(tau + 1) * 4].rearrange("b t c -> t b c"), in_=vst[:, :, :]),
                          reads=[(vst, None)], writes=[(vcb, tau)], dma=True)
                    if own:
                        for half in range(2):
                            rb = panel(w_in_p[6 + half], 8192)
                            for hh in range(4):
                                h = half * 4 + hh
                                bank = dbank()
                                fm_group(rb, hh * 128, K16, lambda kc, UT=UT: UT[:, kc, :], [(UT, None)], bank)
                                evac(lambda h=h: qT[:, h, :], lambda bank=bank: bank[:], [(bank, None)], [(qT, h)])
                                yield

                def gla_gen(tau=tau, own=own):
                    for blk in range(4):
                        tb = ps[4]

                        def ktr(blk=blk, tb=tb):
                            last = None
                            for h in range(4):
                                last = nc.tensor.transpose(out=tb[:].bitcast(BF16)[:, h * 128:(h + 1) * 128],
                                                           in_=kiT[:, h, blk * 128:(blk + 1) * 128], identity=idb[:])
                            return last
                        P.add("pe", ktr, reads=[(kiT, None), (idb, None)], writes=[(tb, None)])
                        P.add("dve", lambda blk=blk, tb=tb: nc.vector.tensor_copy(out=k_tok[:, blk, :], in_=tb[:].bitcast(BF16)[:, 0:512]),
                              reads=[(tb, None)], writes=[(k_tok, blk)])
                        yield
                        if own:
                            for h in range(4):
                                P.add("dve", lambda h=h, blk=blk: nc.vector.tensor_scalar(out=S_bf[:, h, :], in0=Tst[:, h, :], scalar1=decs[:, blk, h:h + 1], scalar2=None, op0=ALU.mult),
                                      reads=[(Tst, h), (decs, None)], writes=[(S_bf, h)])
                            scb = ps[4]

                            def sc_mm(blk=blk, scb=scb):
                                last = None
                                for h in range(4):
                                    last = nc.tensor.matmul(scb[:, h * 128:(h + 1) * 128], kiT[:, h, blk * 128:(blk + 1) * 128],
                                                            qdT[:, h, blk * 128:(blk + 1) * 128], start=True, stop=True)
                                return last
                            P.add("pe", sc_mm, reads=[(kiT, None), (qdT, None)], writes=[(scb, None)])
                            P.add("dve", lambda scb=scb: nc.vector.tensor_tensor(
                                out=scm[:, :, :], in0=scb[:].rearrange("p (h c) -> p h c", h=4),
                                in1=maskC().unsqueeze(1).to_broadcast([128, 4, 128]), op=ALU.mult),
                                reads=[(scb, None), (cst, None)], writes=[(scm, None)])
                            yield
                            for b2 in range(2):
                                ob = ps[5 + b2]

                                def o_mm(b2=b2, ob=ob, blk=blk):
                                    last = None
                                    for i in range(4):
                                        idx = b2 * 4 + i
                                        h, ec = idx // 2, idx % 2
                                        o = ob[:, i * 128:(i + 1) * 128]
                                        nc.tensor.matmul(o, S_bf[:, h, ec * 128:(ec + 1) * 128], qdT[:, h, blk * 128:(blk + 1) * 128], start=True, stop=False)
                                        last = nc.tensor.matmul(o, gv_tok[:, blk, h * 256 + ec * 128: h * 256 + (ec + 1) * 128], scm[:, h, :], start=False, stop=True)
                                    return last
                                P.add("pe", o_mm, reads=[(S_bf, None), (qdT, None), (gv_tok, None), (scm, None)], writes=[(ob, None)])
                                P.add("act", lambda b2=b2, ob=ob: nc.scalar.activation(out=sq[:, b2 * 4:(b2 + 1) * 4, :], in_=ob[:].rearrange("p (a c) -> p a c", a=4), func=AF.Square),
                                      reads=[(ob, None)], writes=[(sq, b2)])
                                yield
                            ssb = ps[4]

                            def ss_mm(ssb=ssb):
                                last = None
                                for h in range(4):
                                    nc.tensor.matmul(ssb[:, h * 128:(h + 1) * 128], onesb[:], sq[:, 2 * h, :], start=True, stop=False)
                                    last = nc.tensor.matmul(ssb[:, h * 128:(h + 1) * 128], onesb[:], sq[:, 2 * h + 1, :], start=False, stop=True)
                                return last
                            P.add("pe", ss_mm, reads=[(onesb, None), (sq, None)], writes=[(ssb, None)])
                            P.add("act", lambda ssb=ssb: nc.scalar.activation(out=rstdg[:, :, :], in_=ssb[:].rearrange("p (h c) -> p h c", h=4), func=AF.Ln, scale=1.0 / 256, bias=EPS),
                                  reads=[(ssb, None)], writes=[(rstdg, None)])
                            P.add("act", lambda: nc.scalar.activation(out=rstdg[:, :, :], in_=rstdg[:, :, :], func=AF.Exp, scale=-0.5),
                                  reads=[(rstdg, None)], writes=[(rstdg, None)])
                            yield
                            for b2 in range(2):
                                ob = ps[5 + b2]
                                for ec in range(2):
                                    P.add("dve", lambda b2=b2, ob=ob, ec=ec: nc.vector.scalar_tensor_tensor(
                                        out=tmp_o[:, :, :].rearrange("p (h e) c -> p h e c", e=2)[:, b2 * 2:(b2 + 1) * 2, ec, :],
                                        in0=ob[:].rearrange("p (h e c) -> p h e c", h=2, e=2)[:, :, ec, :],
                                        scalar=gfm[:, 48 + ec:49 + ec], in1=rstdg[:, b2 * 2:(b2 + 1) * 2, :], op0=ALU.mult, op1=ALU.mult),
                                        reads=[(ob, None), (gfm, None), (rstdg, None)], writes=[(tmp_o, (b2, ec))])
                            P.add("dve", lambda blk=blk: nc.vector.tensor_tensor(out=o_aT[:, :, blk * 128:(blk + 1) * 128], in0=tmp_o[:, :, :],
                                                                                 in1=silu_g[:, :, blk * 128:(blk + 1) * 128], op=ALU.mult),
                                  reads=[(tmp_o, None), (silu_g, None)], writes=[(o_aT, blk)])
                            yield
                        for b2 in range(2):
                            kb_ = ps[7]

                            def kv_mm(b2=b2, kb_=kb_, blk=blk):
                                last = None
                                for i in range(2):
                                    h = b2 * 2 + i
                                    last = nc.tensor.matmul(kb_[:, i * 256:(i + 1) * 256], k_tok[:, blk, h * 128:(h + 1) * 128],
                                                            gv_tok[:, blk, h * 256:(h + 1) * 256], start=True, stop=True)
                                return last
                            P.add("pe", kv_mm, reads=[(k_tok, blk), (gv_tok, None)], writes=[(kb_, None)])
                            for i in range(2):
                                h = b2 * 2 + i
                                P.add("dve", lambda h=h, i=i, kb_=kb_, blk=blk: nc.vector.scalar_tensor_tensor(
                                    out=Tst[:, h, :], in0=Tst[:, h, :], scalar=decs[:, blk, h:h + 1], in1=kb_[:, i * 256:(i + 1) * 256],
                                    op0=ALU.mult, op1=ALU.add), reads=[(Tst, h), (decs, None), (kb_, None)], writes=[(Tst, h)])
                            yield

                gens = [gla_gen(), dense_gen()]
                if (not own) and tau + 1 < ntiles:
                    gens.append(s0_gen(tau + 1))
                rot["nb"] = 2
                interleave(*gens)
                rot["nb"] = 4
                P.add("dve", lambda: nc.vector.tensor_copy(out=decs[:, 0, :], in_=decs[:, 4, :]), reads=[(decs, None)], writes=[(decs, None)])
                if tau == 1:
                    dump("o_aT1", o_aT, lambda: o_aT[:, 0, :], [128, 512])
                if not own:
                    continue
                if stop_after == "S2":
                    break
                scale = float(128 ** -0.5)
                steps = [(kap, kb) for kap in range(tau, -1, -1) for kb in (3, 2, 1, 0)]
                for hp in range(8 // NH):
                    hs = tuple(NH * hp + i for i in range(NH))
                    psZ = tuple(ps[i % 2] for i in range(NH))
                    psC = tuple(ps[2 + i % 2] for i in range(NH))
                    psO = tuple(ps[4 + i] for i in range(NH))
                    for hh in range(NH):
                        h = hs[hh]
                        P.add("dve", lambda hh=hh: nc.vector.memset(SPs[hh][:], 0.0), writes=[(SPs[hh], None)])
                        P.add("pe", lambda hh=hh, h=h: nc.tensor.matmul(psO[hh][:], zerosb[:], qT[:, h, :], start=True, stop=False),
                              reads=[(zerosb, None), (qT, h)], writes=[(psO[hh], None)])
                    def kv_load(kap):
                        sl = kap % 2
                        for hh in range(NH):
                            h = hs[hh]
                            P.add("sp", lambda hh=hh, h=h, kap=kap, sl=sl: nc.sync.dma_start(out=Kt[hh][sl][:], in_=kcache[h, :, kap * 512:(kap + 1) * 512]),
                                  reads=[(kcb, kap)], writes=[(Kt[hh][sl], None)], dma=True)
                            P.add("sp", lambda hh=hh, h=h, kap=kap, sl=sl: nc.sync.dma_start(
                                out=Vt[hh][sl][:, :, :], in_=vcache[kap * 4:(kap + 1) * 4, :, h * 128:(h + 1) * 128].rearrange("b t d -> t b d")),
                                reads=[(vcb, kap)], writes=[(Vt[hh][sl], None)], dma=True)

                    def stage_qk(si):
                        kap, kb = steps[si]
                        diag = kap == tau
                        qlo = kb * 128 if diag else 0
                        sl = kap % 2
                        for hh in range(NH):
                            h = hs[hh]
                            P.add("pe", lambda hh=hh, h=h, sl=sl, kb=kb, qlo=qlo: nc.tensor.matmul(
                                psZ[hh][:, qlo:512], Kt[hh][sl][:, kb * 128:(kb + 1) * 128], qT[:, h, qlo:512], start=True, stop=True),
                                reads=[(Kt[hh][sl], None), (qT, h)], writes=[(psZ[hh], None)])
                            P.add("act", lambda hh=hh, qlo=qlo: nc.scalar.activation(out=Eb[hh][:, qlo:512], in_=psZ[hh][:, qlo:512], func=AF.Exp, scale=scale),
                                  reads=[(psZ[hh], None)], writes=[(Eb[hh], None)])
                            if diag:
                                P.add("dve", lambda hh=hh, qlo=qlo: nc.vector.tensor_tensor(out=Eb[hh][:, qlo:qlo + 128], in0=Eb[hh][:, qlo:qlo + 128], in1=maskS(), op=ALU.mult),
                                      reads=[(Eb[hh], None), (cst, None)], writes=[(Eb[hh], None)])

                    kv_load(tau)
                    if tau >= 1:
                        kv_load(tau - 1)
                    stage_qk(0)
                    for si, (kap, kb) in enumerate(steps):
                        first = si == 0
                        last_ = si == len(steps) - 1
                        diag = kap == tau
                        qlo = kb * 128 if diag else 0
                        sl = kap % 2
                        if kb == 3 and kap < tau and kap >= 1:
                            kv_load(kap - 1)
                        for hh in range(NH):
                            P.add("act", lambda hh=hh, qlo=qlo: nc.scalar.activation(out=SPb[hh][:, qlo:512], in_=Eb[hh][:, qlo:512], func=AF.Ln, bias=1.0),
                                  reads=[(Eb[hh], None)], writes=[(SPb[hh], None)])
                        for pair in range(NH // 2):
                            for hh in (2 * pair, 2 * pair + 1):
                                def c_mm(hh=hh, qlo=qlo, first=first):
                                    i1 = nc.tensor.matmul(psC[hh][:, qlo:512], negUb[:], SPb[hh][:, qlo:512], start=True, stop=first)
                                    if not first:
                                        i1 = nc.tensor.matmul(psC[hh][:, qlo:512], negonesb[:], SPs[hh][:, qlo:512], start=False, stop=True)
                                    return i1
                                P.add("pe", c_mm, reads=[(negUb, None), (SPb[hh], None), (SPs[hh], None), (negonesb, None)], writes=[(psC[hh], None)])
                            for hh in (2 * pair, 2 * pair + 1):
                                P.add("act", lambda hh=hh, qlo=qlo: nc.scalar.activation(out=ECb[hh][:, qlo:512], in_=psC[hh][:, qlo:512], func=AF.Exp),
                                      reads=[(psC[hh], None)], writes=[(ECb[hh], None)])
                        for hh in range(NH):
                            P.add("dve", lambda hh=hh, qlo=qlo: nc.vector.tensor_tensor(out=Ab[hh][:, qlo:512], in0=Eb[hh][:, qlo:512], in1=ECb[hh][:, qlo:512], op=ALU.mult),
                                  reads=[(Eb[hh], None), (ECb[hh], None)], writes=[(Ab[hh], None)])
                            if not last_:
                                P.add("dve", lambda hh=hh, qlo=qlo: nc.vector.tensor_tensor(out=SPs[hh][:, qlo:512], in0=SPs[hh][:, qlo:512], in1=SPb[hh][:, qlo:512], op=ALU.add),
                                      reads=[(SPs[hh], None), (SPb[hh], None)], writes=[(SPs[hh], None)])
                        if not last_:
                            stage_qk(si + 1)
                        for hh in range(NH):
                            P.add("pe", lambda hh=hh, sl=sl, kb=kb, qlo=qlo, last_=last_: nc.tensor.matmul(
                                psO[hh][:, qlo:512], Vt[hh][sl][:, kb, :], Ab[hh][:, qlo:512], start=False, stop=last_),
                                reads=[(Vt[hh][sl], None), (Ab[hh], None)], writes=[(psO[hh], None)])
                    for hh in range(NH):
                        h = hs[hh]
                        P.add("act", lambda hh=hh, h=h: nc.scalar.copy(out=o_bT[:, h, :], in_=psO[hh][:]), reads=[(psO[hh], None)], writes=[(o_bT, h)])
                if tau == 1:
                    dump("o_bT1", o_bT, lambda: o_bT[:, 0, :], [128, 512])
                if stop_after == "S3":
                    break
                for blk in range(4):
                    P.add("sp", lambda tau=tau, blk=blk: nc.sync.dma_start(out=xh[blk][:], in_=xs[tau, blk]), writes=[(xh[blk], None)], dma=True)
                P.add("sp", lambda: nc.sync.dma_start(out=gbc[:], in_=gtm_d[0]), writes=[(gbc, None)], dma=True)
                K8 = list(range(8))
                for j in range(4):
                    rb = panel(w_in_p[12 + j], 8192)
                    for fc in range(4):
                        bank = dbank()
                        fm_group(rb, fc * 128, K16, lambda kc: uT[:, kc, :], [(uT, None)], bank)
                        P.add("act", lambda fc=fc, bank=bank: nc.scalar.activation(out=sga[fc][:], in_=bank[:], func=AF.Sigmoid),
                              reads=[(bank, None)], writes=[(sga[fc], None)])
                    rb = panel(w_bg[j], 4096)
                    for fc in range(4):
                        bank = dbank()
                        fm_group(rb, fc * 128, K8, lambda kc: o_aT[:, kc, :], [(o_aT, None)], bank)
                        P.add("dve", lambda fc=fc, bank=bank: nc.vector.tensor_tensor(out=sga[fc][:], in0=bank[:], in1=sga[fc][:], op=ALU.mult),
                              reads=[(bank, None), (sga[fc], None)], writes=[(sga[fc], None)])
                    rb = panel(w_in_p[16 + j], 8192)
                    for fc in range(4):
                        bank = dbank()
                        fm_group(rb, fc * 128, K16, lambda kc: uT[:, kc, :], [(uT, None)], bank)
                        P.add("act", lambda fc=fc, bank=bank: nc.scalar.activation(out=sgb[fc][:], in_=bank[:], func=AF.Sigmoid),
                              reads=[(bank, None)], writes=[(sgb[fc], None)])
                    rb = panel(w_bs[j], 4096)
                    for fc in range(4):
                        bank = dbank()
                        fm_group(rb, fc * 128, K8, lambda kc: o_bT[:, kc, :], [(o_bT, None)], bank)
                        P.add("dve", lambda fc=fc, bank=bank: nc.vector.tensor_tensor(out=sgb[fc][:], in0=bank[:], in1=sgb[fc][:], op=ALU.mult),
                              reads=[(bank, None), (sgb[fc], None)], writes=[(sgb[fc], None)])
                        P.add("dve", lambda fc=fc, j=j: nc.vector.tensor_tensor(out=yT[:, 4 * j + fc, :], in0=sga[fc][:], in1=sgb[fc][:], op=ALU.add),
                              reads=[(sga[fc], None), (sgb[fc], None)], writes=[(yT, 4 * j + fc)])
                if stop_after == "S4a":
                    break
                if tau == 1:
                    dump("yT1", yT, lambda: yT[:, 0, :], [128, 512])
                for j in range(4):
                    rb = panel(w_out_p[j], 8192)
                    for blk in range(4):
                        bank = dbank()
                        tm_group(rb, K16, lambda kc, blk=blk: yT[:, kc, blk * 128:(blk + 1) * 128], [(yT, None)], bank)
                        P.add("dve", lambda blk=blk, j=j, bank=bank: nc.vector.tensor_copy(out=m_sb[blk][:, j * 512:(j + 1) * 512], in_=bank[:]),
                              reads=[(bank, None)], writes=[(m_sb[blk], j)])

                def post_norm_add(srcs, gb, gkey):
                    for blk in range(4):
                        c0 = 8 + 3 * (blk % 2)
                        jn = xn[blk % 2]
                        P.add("act", lambda blk=blk, jn=jn, c0=c0: nc.scalar.activation(out=jn[:], in_=srcs[blk][:], func=AF.Square, accum_out=sst[:, c0:c0 + 1]),
                              reads=[(srcs[blk], None)], writes=[(jn, None), (sst, c0)])
                        P.add("act", lambda c0=c0: nc.scalar.activation(out=sst[:, c0 + 1:c0 + 2], in_=sst[:, c0:c0 + 1], func=AF.Ln, scale=1.0 / 2048, bias=EPS),
                              reads=[(sst, c0)], writes=[(sst, c0 + 1)])
                        P.add("act", lambda c0=c0: nc.scalar.activation(out=sst[:, c0 + 2:c0 + 3], in_=sst[:, c0 + 1:c0 + 2], func=AF.Exp, scale=-0.5),
                              reads=[(sst, c0 + 1)], writes=[(sst, c0 + 2)])
                        P.add("dve", lambda blk=blk, c0=c0: nc.vector.scalar_tensor_tensor(out=srcs[blk][:], in0=srcs[blk][:], scalar=sst[:, c0 + 2:c0 + 3], in1=gb[:],
                                                                                  op0=ALU.mult, op1=ALU.mult),
                              reads=[(srcs[blk], None), (sst, c0 + 2), (gb, None)], writes=[(srcs[blk], None)])
                        P.add("pool", lambda blk=blk: nc.gpsimd.tensor_tensor(out=xh[blk][:], in0=xh[blk][:], in1=srcs[blk][:], op=ALU.add),
                              reads=[(xh[blk], None), (srcs[blk], None)], writes=[(xh[blk], None)])
                post_norm_add(m_sb, gbc, 0)
                if tau == 1:
                    dump("h1", xh[0], lambda: xh[0][:, 0:512], [128, 512])
                if stop_after == "S4":
                    break
                for blk in range(4):
                    norm_transpose(xh[blk], lambda blk=blk: xh[blk][:], 16, blk)
                for half in range(2):
                    for pj in range(8):
                        rb = panel(w_up_p[half * 8 + pj], 8192)
                        for fc in range(4):
                            ci = pj * 4 + fc
                            bank = dbank()
                            fm_group(rb, fc * 128, K16, lambda kc: uT[:, kc, :], [(uT, None)], bank)
                            rt = rtmp[ci % 2]
                            P.add("act", lambda bank=bank, rt=rt: nc.scalar.activation(out=rt[:], in_=bank[:], func=AF.Relu),
                                  reads=[(bank, None)], writes=[(rt, None)])
                            P.add("dve", lambda ci=ci, rt=rt: nc.vector.tensor_tensor(out=aT[:, ci, :], in0=rt[:], in1=rt[:], op=ALU.mult),
                                  reads=[(rt, None)], writes=[(aT, ci)])
                    for j in range(4):
                        for s2 in range(2):
                            rb = panel(w_dn_p[j, half * 2 + s2], 8192)
                            for blk in range(4):
                                bank = ps[4 + blk]
                                tm_group(rb, K16, lambda kc, blk=blk, s2=s2: aT[:, s2 * 16 + kc, blk * 128:(blk + 1) * 128], [(aT, None)], bank,
                                         start=(s2 == 0), stop=(s2 == 1))
                        for blk in range(4):
                            bank = ps[4 + blk]
                            if half == 0:
                                P.add("dve", lambda blk=blk, j=j, bank=bank: nc.vector.tensor_copy(out=f_sb[blk][:, j * 512:(j + 1) * 512], in_=bank[:]),
                                      reads=[(bank, None)], writes=[(f_sb[blk], j)])
                            else:
                                P.add("dve", lambda blk=blk, j=j, bank=bank: nc.vector.tensor_tensor(out=f_sb[blk][:, j * 512:(j + 1) * 512], in0=f_sb[blk][:, j * 512:(j + 1) * 512], in1=bank[:], op=ALU.add),
                                      reads=[(bank, None), (f_sb[blk], j)], writes=[(f_sb[blk], j)])
                P.add("sp", lambda: nc.sync.dma_start(out=gbc2[:], in_=gtm_d[1]), writes=[(gbc2, None)], dma=True)
                post_norm_add(f_sb, gbc2, 1)
                if tau == 1:
                    dump("h2", xh[0], lambda: xh[0][:, 0:512], [128, 512])
                if stop_after == "S5":
                    break
                for blk in range(4):
                    norm_transpose(xh[blk], lambda blk=blk: xh[blk][:], 32, blk)
                P.add("sp", lambda oj=oj: nc.sync.dma_start(out=p_f[:, :, :], in_=pp[oj].rearrange("b t c -> t b c")), writes=[(p_f, None)], dma=True)
                P.add("dve", lambda: nc.vector.tensor_copy(out=p_b[:, :, :], in_=p_f[:, :, :]), reads=[(p_f, None)], writes=[(p_b, None)])
                for blk in range(4):
                    tb = ps[0]

                    def ptr(blk=blk, tb=tb):
                        last = None
                        for kc in range(2):
                            last = nc.tensor.transpose(out=tb[:].bitcast(BF16)[:, kc * 128:(kc + 1) * 128], in_=p_b[:, blk, kc * 128:(kc + 1) * 128], identity=idb[:])
                        return last
                    P.add("pe", ptr, reads=[(p_b, None), (idb, None)], writes=[(tb, None)])
                    P.add("dve", lambda blk=blk, tb=tb: nc.vector.tensor_copy(out=pT[:, :, blk * 128:(blk + 1) * 128], in_=tb[:].bitcast(BF16)[:, 0:256].rearrange("p (a b) -> p a b", a=2)),
                          reads=[(tb, None)], writes=[(pT, blk)])
                def s6_gen(oj=oj):
                    for j in range(4):
                        rbg = panel(w_pg_p[j], 8192)
                        for blk in range(4):
                            gbk = ps[blk % 2]
                            tm_group(rbg, K16, lambda kc, blk=blk: uT[:, kc, blk * 128:(blk + 1) * 128], [(uT, None)], gbk)
                            P.add("act", lambda gbk=gbk, blk=blk: nc.scalar.activation(out=sgp[blk][:], in_=gbk[:], func=AF.Sigmoid), reads=[(gbk, None)], writes=[(sgp[blk], None)])
                            yield
                        rbp = panel(w_pp_p[j], 1024)
                        for blk in range(4):
                            ebk = ps[4 + blk % 2]
                            tm_group(rbp, [0, 1], lambda kc, blk=blk: pT[:, kc, blk * 128:(blk + 1) * 128], [(pT, None)], ebk)
                            P.add("dve", lambda ebk=ebk, blk=blk: nc.vector.tensor_tensor(out=sgp[blk][:], in0=ebk[:], in1=sgp[blk][:], op=ALU.mult),
                                  reads=[(ebk, None), (sgp[blk], None)], writes=[(sgp[blk], None)])
                            P.add("pool", lambda blk=blk, j=j: nc.gpsimd.tensor_tensor(out=xh[blk][:, j * 512:(j + 1) * 512], in0=xh[blk][:, j * 512:(j + 1) * 512], in1=sgp[blk][:], op=ALU.add),
                                  reads=[(xh[blk], None), (sgp[blk], None)], writes=[(xh[blk], None)])
                            yield
                    for blk in range(4):
                        P.add("sp", lambda oj=oj, blk=blk: nc.sync.dma_start(out=y[oj, blk], in_=xh[blk][:]), reads=[(xh[blk], None)], dma=True)

                gens = [s6_gen()]
                if tau + 1 < ntiles:
                    gens.append(s0_gen(tau + 1))
                interleave(*gens)

        sstm = sb("sstm", [128, 16], F32)
        P0 = Prog(nc, spaces, plan=True)
        run(P0)
        P1 = Prog(nc, spaces, plan=False)
        P1.panel_list = P0.panel_log
        run(P1)
        nw = P1.emit(es)
        DEBUG["nwaits"] = nw
        DEBUG["nops"] = P1.nops
        DEBUG["npanels"] = len(P0.panel_log)
    return nc, dbg_out


def _panelize(W):
    K, N = W.shape
    return np.ascontiguousarray(W.reshape(K // 128, 128, N // 512, 512).transpose(2, 1, 0, 3).reshape(N // 512, 128, (K // 128) * 512))


def prep_shared(inp):
    f = lambda a: np.asarray(a, dtype=np.float32)
    w_in = f(inp["w_in"])[0]
    cols = {"gq": (0, 512), "gk": (512, 512), "gv": (1024, 1024), "glr": (2048, 16), "gout": (2064, 1024), "sq": (3088, 1024),
            "sk": (4112, 1024), "sv": (5136, 1024), "ga": (6160, 2048), "gb": (8208, 2048)}
    order = ["gq", "gk", "gv", "gout", "sq", "sk", "sv", "ga", "gb"]
    w_in_p = np.concatenate([_panelize(w_in[:, cols[k][0]:cols[k][0] + cols[k][1]]) for k in order], 0)
    assert w_in_p.shape == (20, 128, 8192)
    glr = w_in[:, 2048:2064]
    w_glr = np.ascontiguousarray(glr.reshape(16, 128, 16).transpose(1, 0, 2).reshape(128, 256))
    wg1 = np.zeros((32, 512), np.float32)
    wg1[0:16] = f(inp["w_gate_up"])[0]
    wg1[16] = f(inp["b_gate"])[0]
    w_dn = f(inp["w_mlp_down"])[0]
    w_dn_p = np.ascontiguousarray(w_dn.reshape(4, 16, 128, 4, 512).transpose(3, 0, 2, 1, 4).reshape(4, 4, 128, 8192))
    gfm = np.zeros((128, 64), np.float32)
    gfm[:, 0:16] = f(inp["norm_mix_pre"])[0].reshape(16, 128).T
    gfm[:, 16:32] = f(inp["norm_mlp_pre"])[0].reshape(16, 128).T
    gfm[:, 32:48] = f(inp["norm_ple"])[0].reshape(16, 128).T
    gfm[:, 48:50] = f(inp["gla_norm"])[0].reshape(2, 128).T
    gtm = np.stack([np.tile(f(inp["norm_mix_post"])[0][None, :], (128, 1)), np.tile(f(inp["norm_mlp_post"])[0][None, :], (128, 1))], 0)
    i = np.arange(128)
    consts = np.zeros((128, 640), np.float32)
    consts[:, 0:128] = np.eye(128)
    consts[:, 128:256] = (i[:, None] < i[None, :])
    consts[:, 256:384] = -1.0 * (i[:, None] >= i[None, :])
    consts[:, 384:512] = (-1.0 / 16.0) * (i[:, None] <= i[None, :])
    consts[:, 512:640] = (i[:, None] <= i[None, :])
    return {
        "w_in_p": w_in_p, "w_glr": w_glr, "wg1": wg1,
        "w_bg": _panelize(f(inp["w_branch_gla"])[0]), "w_bs": _panelize(f(inp["w_branch_sb"])[0]),
        "w_out_p": _panelize(f(inp["w_out"])[0]), "w_up_p": _panelize(f(inp["w_mlp_up"])[0]), "w_dn_p": w_dn_p,
        "w_pg_p": _panelize(f(inp["w_ple_gate"])[0]), "w_pp_p": _panelize(f(inp["w_ple_proj"])[0]),
        "gfm": gfm, "gtm": np.ascontiguousarray(gtm), "consts": consts,
    }


def prep_core(inp, b, c):
    x = np.asarray(inp["x"], dtype=np.float32)
    p = np.asarray(inp["p"], dtype=np.float32)
    xs = np.zeros((NT, 512, 2048), np.float32)
    pp = np.zeros((4, 512, 256), np.float32)
    for j in range(4):
        if c == 1:
            xs[2 * j] = x[b, (2 * j) * 512:(2 * j + 1) * 512]
        elif j >= 1:
            xs[2 * j] = x[b, (2 * j - 1) * 512:(2 * j) * 512]
        g = 2 * j + c
        xs[2 * j + 1] = x[b, g * 512:(g + 1) * 512]
        pp[j] = p[0, b, g * 512:(g + 1) * 512]
    return {"xs": xs.reshape(NT, 4, 128, 2048), "pp": pp.reshape(4, 4, 128, 256)}


_CACHE = {}


def kernel(**inputs):
    if "nc" not in _CACHE:
        _CACHE["nc"] = build()[0]
    nc = _CACHE["nc"]
    shared = prep_shared(inputs)
    in_maps = []
    for core in range(8):
        b, c = core // 2, core % 2
        m = dict(shared)
        m.update(prep_core(inputs, b, c))
        in_maps.append(m)
    res = run_bass_kernel_spmd(nc, in_maps, core_ids=list(range(8)))
    out = np.zeros((4, 4096, 2048), np.float32)
    for core in range(8):
        b, c = core // 2, core % 2
        yy = np.asarray(res.results[core]["y"]).reshape(4, 512, 2048)
        for j in range(4):
            g = 2 * j + c
            out[b, g * 512:(g + 1) * 512] = yy[j]
    return out
```

```python
import numpy as np
import concourse.bass as bass
import concourse.mybir as mybir
from concourse.bass_utils import run_bass_kernel_spmd
from contextlib import ExitStack

F32 = mybir.dt.float32
BF16 = mybir.dt.bfloat16
AF = mybir.ActivationFunctionType
ALU = mybir.AluOpType
AX = mybir.AxisListType

class Buf:
    def __init__(self, name, t, space=None, lo=0, hi=1):
        self.name = name
        self.t = t
        self.space = space if space is not None else name
        self.lo = lo
        self.hi = hi
        self.entries = {}

    def __getitem__(self, k):
        return self.t[k]


class Op:
    __slots__ = ("eng", "fn", "deps", "dma", "idx", "lidx", "sem", "val", "milestone", "mcount", "waits")

    def __init__(self, eng, fn, dma):
        self.eng = eng
        self.fn = fn
        self.deps = {}
        self.dma = dma
        self.sem = None
        self.val = 0
        self.milestone = False
        self.mcount = 0
        self.waits = None


class Prog:
    ENGS = ("pe", "act", "dve", "pool", "sp")
    NDMA = 12

    def __init__(self, nc, spaces=None, plan=False):
        self.nc = nc
        self.plan = plan
        self.ops = {e: [] for e in self.ENGS}
        self.nops = 0
        self.spaces = spaces if spaces is not None else {}
        self.panel_list = []
        self.dma_count = {"sp": 0, "pool": 0, "act": 0}
        self.dma_ops = {"sp": [], "pool": [], "act": []}
        self.panel_log = []

    def buf(self, name, t, space=None, lo=0, hi=1):
        b = Buf(name, t, space, lo, hi)
        self.spaces.setdefault(b.space, []).append(b)
        return b

    def _conf_entries(self, b, k):
        for k2, e in b.entries.items():
            if k is None or k2 is None or k2 == k:
                yield e
        sp = self.spaces[b.space]
        if len(sp) > 1:
            for b2 in sp:
                if b2 is not b and b2.lo < b.hi and b.lo < b2.hi:
                    for e in b2.entries.values():
                        yield e

    def add(self, eng, fn, reads=(), writes=(), dma=False):
        if self.plan:
            return None
        op = Op(eng, fn, dma)
        op.idx = self.nops
        self.nops += 1
        op.lidx = len(self.ops[eng])
        deps = op.deps
        for (b, k) in reads:
            for e in self._conf_entries(b, k):
                if e[0] is not None:
                    deps[e[0]] = True
        for (b, k) in writes:
            for e in self._conf_entries(b, k):
                if e[0] is not None and e[0] not in deps:
                    deps[e[0]] = False
                for r in e[1].values():
                    if r not in deps:
                        deps[r] = False
                for r in e[2]:
                    if r not in deps:
                        deps[r] = False
        for (b, k) in reads:
            e = b.entries.get(k)
            if e is None:
                e = b.entries[k] = [None, {}, []]
            if dma:
                e[2].append(op)
            else:
                e[1][eng] = op
        for (b, k) in writes:
            if k is None:
                for kk in list(b.entries.keys()):
                    if kk is not None:
                        del b.entries[kk]
            b.entries[k] = [op, {}, []]
        deps.pop(op, None)
        if dma:
            q = eng
            n = self.dma_count[q]
            self.dma_count[q] += 1
            op.sem = (q, n % self.NDMA)
            op.val = 16 * (n // self.NDMA + 1)
            if n >= self.NDMA:
                deps[self.dma_ops[q][n - self.NDMA]] = True
            self.dma_ops[q].append(op)
        self.ops[eng].append(op)
        return op

    def emit(self, es):
        nc = self.nc
        engobj = {"pe": nc.tensor, "act": nc.scalar, "dve": nc.vector, "pool": nc.gpsimd, "sp": nc.sync}
        esem = {e: es.enter_context(nc.semaphore("sem_" + e)) for e in ("pe", "act", "dve", "pool")}
        dsem = {}
        for q in ("sp", "pool", "act"):
            if self.dma_count[q]:
                for i in range(min(self.NDMA, self.dma_count[q])):
                    dsem[(q, i)] = es.enter_context(nc.semaphore("dsem_%s_%d" % (q, i)))
        for E in self.ENGS:
            seen = {}
            seend = {}
            for op in self.ops[E]:
                w = []
                for p, raw in op.deps.items():
                    if p.dma:
                        if seend.get(p.sem, 0) >= p.val:
                            continue
                        seend[p.sem] = p.val
                        w.append(p)
                    else:
                        if p.eng == E:
                            if E == "pe" or not raw:
                                continue
                        if seen.get(p.eng, -1) >= p.lidx:
                            continue
                        seen[p.eng] = p.lidx
                        p.milestone = True
                        w.append(p)
                op.waits = w
        for E in self.ENGS:
            c = 0
            for op in self.ops[E]:
                if op.milestone and not op.dma:
                    c += 1
                op.mcount = c
        nwait = 0
        for E in self.ENGS:
            eo = engobj[E]
            for op in self.ops[E]:
                best = {}
                for p in op.waits:
                    if p.dma:
                        key = dsem[p.sem]
                        v = p.val
                    else:
                        key = esem[p.eng]
                        v = p.mcount
                    if best.get(key, (0, None))[0] < v:
                        best[key] = (v, key)
                for v, key in best.values():
                    eo.wait_ge(key, v)
                    nwait += 1
                inst = op.fn()
                if op.dma:
                    inst.then_inc(dsem[op.sem], 16)
                elif op.milestone:
                    inst.then_inc(esem[E], 1)
        for q in ("sp", "pool", "act"):
            n = self.dma_count[q]
            for i in range(min(self.NDMA, n)):
                cnt = (n - 1 - i) // self.NDMA + 1
                nc.sync.wait_ge(dsem[(q, i)], 16 * cnt)
        return nwait

NT = 8
TT = 512
EPS = 1e-6
W_IN_ORDER_OTHER = [1, 2, 3, 8, 9, 10, 11]
DEBUG = {}


def build(ntiles=NT, stop_after=None, dbg=()):
    nc = bass.Bass("TRN2", target_bir_lowering=False)

    def dram(name, shape, dt=F32, kind="ExternalInput"):
        return nc.dram_tensor(name, shape, dt, kind=kind).ap()

    xs = dram("xs", [NT, 4, 128, 2048])
    pp = dram("pp", [4, 4, 128, 256])
    w_in_p = dram("w_in_p", [20, 128, 8192])
    w_glr = dram("w_glr", [128, 256])
    wg1 = dram("wg1", [32, 512])
    w_bg = dram("w_bg", [4, 128, 4096])
    w_bs = dram("w_bs", [4, 128, 4096])
    w_out_p = dram("w_out_p", [4, 128, 8192])
    w_up_p = dram("w_up_p", [16, 128, 8192])
    w_dn_p = dram("w_dn_p", [4, 4, 128, 8192])
    w_pg_p = dram("w_pg_p", [4, 128, 8192])
    w_pp_p = dram("w_pp_p", [4, 128, 1024])
    gfm_d = dram("gfm", [128, 64])
    gtm_d = dram("gtm", [2, 128, 2048])
    consts_d = dram("consts", [128, 640])
    y = dram("y", [4, 4, 128, 2048], F32, "ExternalOutput")
    kcache = dram("kcache", [8, 128, NT * TT], BF16, "Internal")
    vcache = dram("vcache", [NT * 4, 128, 1024], BF16, "Internal")
    dbg_out = {}

    with ExitStack() as es:
        e = es.enter_context
        spaces = {}

        def sb(name, shape, dt):
            t = e(nc.sbuf_tensor(name, shape, dt))
            b = Buf(name, t)
            spaces.setdefault(b.space, []).append(b)
            return b

        ring = [sb("ring%d" % i, [128, 8192], BF16) for i in range(3)]
        xh_t = e(nc.sbuf_tensor("xh_all", [128, 8192], F32))
        xh = []
        for i in range(4):
            b = Buf("xh%d" % i, xh_t[:, i * 2048:(i + 1) * 2048], "xh", i * 8192, (i + 1) * 8192)
            spaces.setdefault("xh", []).append(b)
            xh.append(b)
        gsa = Buf("gsa", xh_t[:, 0:4096].bitcast(BF16).rearrange("p (a b) -> p a b", a=16), "xh", 0, 16384)
        gsb = Buf("gsb", xh_t[:, 4096:8192].bitcast(BF16).rearrange("p (a b) -> p a b", a=16), "xh", 16384, 32768)
        spaces["xh"].extend([gsa, gsb])
        xin = [sb("xin%d" % i, [128, 2048], F32) for i in range(2)]
        xn = [sb("xn%d" % i, [128, 2048], BF16) for i in range(2)]
        uT = sb("uT", [128, 16, 512], BF16)
        Tst = sb("Tst", [128, 4, 256], F32)
        S_bf = sb("S_bf", [128, 4, 256], BF16)
        decs = sb("decs", [128, 5, 4], F32)
        gfm = sb("gfm_sb", [128, 64], F32)
        cst = sb("cst", [128, 640], F32)
        idb = sb("idb", [128, 128], BF16)
        onesb = sb("onesb", [128, 128], BF16)
        zerosb = sb("zerosb", [128, 128], BF16)
        negonesb = sb("negonesb", [128, 128], BF16)
        negUb = sb("negUb", [128, 128], BF16)
        wglr = sb("wglr", [128, 256], BF16)
        wg1sb = sb("wg1sb", [32, 512], F32)
        glrT1 = sb("glrT1", [32, 512], F32)
        sst = sb("sst", [128, 16], F32)
        Ut = e(nc.sbuf_tensor("U", [128, 36864], BF16))
        ps = []
        for i in range(8):
            t = e(nc.psum_tensor("ps%d" % i, [128, 512], F32))
            b = Buf("ps%d" % i, t)
            spaces.setdefault(b.space, []).append(b)
            ps.append(b)
        kcb = Buf("kcache", None)
        vcb = Buf("vcache", None)
        spaces["kcache"] = [kcb]
        spaces["vcache"] = [vcb]

        def carve(name, off_kb, shape, dt):
            nel = int(np.prod(shape[1:]))
            nbytes = nel * (4 if dt == F32 else 2)
            lo = int(off_kb * 1024)
            assert lo + nbytes <= 36864 * 2, name
            v = Ut[:, lo // 2: (lo + nbytes) // 2]
            if dt == F32:
                v = v.bitcast(F32)
            if len(shape) == 3:
                v = v.rearrange("p (a b) -> p a b", a=shape[1])
            elif len(shape) == 4:
                v = v.rearrange("p (a b c) -> p a b c", a=shape[1], b=shape[2])
            b = Buf(name, v, "U", lo, lo + nbytes)
            spaces.setdefault("U", []).append(b)
            return b

        sp_tok = carve("sp_tok", 0, [128, 4, 512], F32)
        eb = carve("eb", 8, [128, 4, 512], F32)
        enb = carve("enb", 16, [128, 4, 512], F32)
        kiT = carve("kiT", 24, [128, 4, 512], BF16)
        qdT = carve("qdT", 52, [128, 4, 512], BF16)
        gv_tok = carve("gv_tok", 28, [128, 4, 1024], BF16)
        kst = carve("kst", 36, [128, 8, 512], BF16)
        vst = carve("vst", 44, [128, 4, 1024], BF16)
        silu_g = carve("silu_g", 56, [128, 8, 512], BF16)
        qT = carve("qT", 64, [128, 8, 512], BF16)
        uT_alt = carve("uT_alt", 56, [128, 16, 512], BF16)
        o_aT = carve("o_aT", 0, [128, 8, 512], BF16)
        k_tok = carve("k_tok", 8, [128, 4, 512], BF16)
        scm = carve("scm", 12, [128, 4, 128], BF16)
        sq = carve("sq", 13, [128, 8, 128], BF16)
        rstdg = carve("rstdg", 16, [128, 4, 128], F32)
        tmp_o = carve("tmp_o", 18, [128, 8, 128], F32)
        o_bT = carve("o_bT", 56, [128, 8, 512], BF16)
        NH = 4
        Eb = [carve("E%d" % i, 8 + 11 * i + 0, [128, 512], F32) for i in range(NH)]
        SPb = [carve("SP%d" % i, 8 + 11 * i + 2, [128, 512], BF16) for i in range(NH)]
        SPs = [carve("SPs%d" % i, 8 + 11 * i + 3, [128, 512], BF16) for i in range(NH)]
        ECb = [carve("EC%d" % i, 8 + 11 * i + 4, [128, 512], F32) for i in range(NH)]
        Ab = [carve("A%d" % i, 8 + 11 * i + 6, [128, 512], BF16) for i in range(NH)]
        Kt = [[carve("Kt%d%d" % (i, j), 8 + 11 * i + 7 + j, [128, 512], BF16) for j in range(2)] for i in range(NH)]
        Vt = [[carve("Vt%d%d" % (i, j), 8 + 11 * i + 9 + j, [128, 4, 128], BF16) for j in range(2)] for i in range(NH)]
        gtmp = [carve("gtmp%d" % i, 52 + 2 * i, [128, 512], F32) for i in range(2)]
        yT = carve("yT", 8, [128, 16, 512], BF16)
        sga = [carve("sga%d" % i, 24 + 2 * i, [128, 512], F32) for i in range(4)]
        sgb = [carve("sgb%d" % i, 32 + 2 * i, [128, 512], F32) for i in range(4)]
        m_sb = [carve("m_sb0", 40, [128, 2048], F32), carve("m_sb1", 48, [128, 2048], F32),
                carve("m_sb2", 0, [128, 2048], F32), carve("m_sb3", 56, [128, 2048], F32)]
        gbc = carve("gbc", 64, [128, 2048], F32)
        aT = carve("aT", 0, [128, 32, 512], BF16)
        f_sb = [carve("f_sb%d" % i, 32 + 8 * i, [128, 2048], F32) for i in range(4)]
        rtmp = [carve("rtmp%d" % i, 64 + 2 * i, [128, 512], F32) for i in range(2)]
        gbc2 = carve("gbc2", 0, [128, 2048], F32)
        p_f = carve("p_f", 0, [128, 4, 256], F32)
        p_b = carve("p_b", 4, [128, 4, 256], BF16)
        pT = carve("pT", 6, [128, 2, 512], BF16)
        sgp = [carve("sgp%d" % i, 8 + 2 * i, [128, 512], F32) for i in range(4)]

        def run(P):
            plan = P.plan
            ring_state = {"n": 0, "issued": 0}

            def panel(ap, nel):
                i = ring_state["n"]
                ring_state["n"] += 1
                if plan:
                    P.panel_log.append((ap, nel))
                    return ring[i % 3]
                while ring_state["issued"] < min(i + 3, len(P.panel_list)):
                    j = ring_state["issued"]
                    apj, nelj = P.panel_list[j]
                    rb = ring[j % 3]
                    P.add("pool", lambda apj=apj, nelj=nelj, rb=rb: nc.gpsimd.dma_start(out=rb[:, 0:nelj], in_=apj),
                          writes=[(rb, None)], dma=True)
                    ring_state["issued"] += 1
                return ring[i % 3]

            rot = {"d": 0, "ev": 0, "nb": 4}

            def dbank():
                rot["d"] = (rot["d"] + 1) % rot["nb"]
                return ps[rot["d"]]

            def interleave(*gens):
                alive = list(gens)
                while alive:
                    for g in list(alive):
                        try:
                            next(g)
                        except StopIteration:
                            alive.remove(g)

            def evac(out_fn, in_fn, reads, writes, eng=None):
                if eng is None:
                    rot["ev"] ^= 1
                    eng = "act" if rot["ev"] else "dve"
                if eng == "act":
                    P.add("act", lambda: nc.scalar.copy(out=out_fn(), in_=in_fn()), reads=reads, writes=writes)
                else:
                    P.add("dve", lambda: nc.vector.tensor_copy(out=out_fn(), in_=in_fn()), reads=reads, writes=writes)

            def dump(name, buf, apfn, shape):
                if name in dbg and not plan:
                    if name not in dbg_out:
                        dbg_out[name] = dram("dbg_" + name, shape, F32, "ExternalOutput")
                    stage = e(nc.sbuf_tensor("dbgs_" + name, shape, F32))
                    sbf = Buf("dbgs_" + name, stage)
                    spaces[sbf.space] = [sbf]
                    P.add("dve", lambda: nc.vector.tensor_copy(out=stage[:], in_=apfn()), reads=[(buf, None)], writes=[(sbf, None)])
                    P.add("sp", lambda: nc.sync.dma_start(out=dbg_out[name], in_=stage[:]), reads=[(sbf, None)], dma=True)

            P.add("sp", lambda: nc.sync.dma_start(out=gfm[:], in_=gfm_d), writes=[(gfm, None)], dma=True)
            P.add("sp", lambda: nc.sync.dma_start(out=cst[:], in_=consts_d), writes=[(cst, None)], dma=True)
            P.add("sp", lambda: nc.sync.dma_start(out=wg1sb[:], in_=wg1), writes=[(wg1sb, None)], dma=True)
            P.add("pool", lambda: nc.gpsimd.dma_start(out=wglr[:], in_=w_glr), writes=[(wglr, None)], dma=True)
            P.add("dve", lambda: nc.vector.tensor_copy(out=idb[:], in_=cst[:, 0:128]), reads=[(cst, None)], writes=[(idb, None)])
            P.add("dve", lambda: nc.vector.memset(onesb[:], 1.0), writes=[(onesb, None)])
            P.add("dve", lambda: nc.vector.memset(zerosb[:], 0.0), writes=[(zerosb, None)])
            P.add("dve", lambda: nc.vector.memset(negonesb[:], -1.0), writes=[(negonesb, None)])
            P.add("dve", lambda: nc.vector.tensor_copy(out=negUb[:], in_=cst[:, 256:384]), reads=[(cst, None)], writes=[(negUb, None)])
            P.add("dve", lambda: nc.vector.memset(glrT1[:], 1.0), writes=[(glrT1, None)])
            P.add("dve", lambda: nc.vector.memset(Tst[:], 0.0), writes=[(Tst, None)])
            P.add("dve", lambda: nc.vector.memset(decs[:], 1.0), writes=[(decs, None)])
            maskS = lambda: cst[:, 128:256]
            negU = lambda: cst[:, 256:384]
            triN = lambda: cst[:, 384:512]
            maskC = lambda: cst[:, 512:640]

            def norm_transpose_g(src_buf, src_fn, gcol, blk, dst=None, banks=None):
                dst = uT if dst is None else dst
                xnb = xn[blk % 2]
                c0 = 3 * (blk % 2)
                P.add("act", lambda: nc.scalar.activation(out=xnb[:], in_=src_fn(), func=AF.Square, accum_out=sst[:, c0:c0 + 1]),
                      reads=[(src_buf, None)], writes=[(xnb, None), (sst, c0)])
                P.add("act", lambda: nc.scalar.activation(out=sst[:, c0 + 1:c0 + 2], in_=sst[:, c0:c0 + 1], func=AF.Ln, scale=1.0 / 2048, bias=EPS),
                      reads=[(sst, c0)], writes=[(sst, c0 + 1)])
                P.add("act", lambda: nc.scalar.activation(out=sst[:, c0 + 2:c0 + 3], in_=sst[:, c0 + 1:c0 + 2], func=AF.Exp, scale=-0.5),
                      reads=[(sst, c0 + 1)], writes=[(sst, c0 + 2)])
                P.add("act", lambda: nc.scalar.activation(out=xnb[:], in_=src_fn(), func=AF.Copy, scale=sst[:, c0 + 2:c0 + 3]),
                      reads=[(src_buf, None), (sst, c0 + 2)], writes=[(xnb, None)])
                yield
                for half in range(2):
                    bank = ps[2 * (blk % 2) + half] if banks is None else banks[half]

                    def tr(half=half, bank=bank):
                        last = None
                        for k in range(8):
                            kc = half * 8 + k
                            last = nc.tensor.transpose(out=bank[:].bitcast(BF16)[:, k * 128:(k + 1) * 128],
                                                       in_=xnb[:, kc * 128:(kc + 1) * 128], identity=idb[:])
                        return last
                    P.add("pe", tr, reads=[(xnb, None), (idb, None)], writes=[(bank, None)])
                    P.add("dve", lambda half=half, bank=bank: nc.vector.tensor_tensor(
                        out=dst[:, half * 8:(half + 1) * 8, blk * 128:(blk + 1) * 128],
                        in0=bank[:].bitcast(BF16).rearrange("p (a b) -> p a b", a=8),
                        in1=gfm[:, gcol + half * 8: gcol + half * 8 + 8].unsqueeze(2).to_broadcast([128, 8, 128]),
                        op=ALU.mult), reads=[(bank, None), (gfm, None)], writes=[(dst, blk)])
                    yield

            def norm_transpose(src_buf, src_fn, gcol, blk):
                for _ in norm_transpose_g(src_buf, src_fn, gcol, blk):
                    pass

            def s0_gen(tau_):
                dst = uT if tau_ % 2 == 1 else uT_alt
                for blk in range(4):
                    xb_ = xin[blk % 2]
                    P.add("sp", lambda tau_=tau_, blk=blk, xb_=xb_: nc.sync.dma_start(out=xb_[:], in_=xs[tau_, blk]),
                          writes=[(xb_, None)], dma=True)
                    yield
                    yield from norm_transpose_g(xb_, lambda xb_=xb_: xb_[:], 0, blk, dst=dst, banks=(ps[2], ps[3]))

            def fm_group(rb, ncol0, kcs, rhs_fn, rhs_reads, bank, ncols=512, m=128, out_fn=None):
                def f():
                    last = None
                    o = out_fn() if out_fn else bank[:, 0:ncols]
                    for i, kc in enumerate(kcs):
                        last = nc.tensor.matmul(o, rb[:, kc * 512 + ncol0: kc * 512 + ncol0 + m], rhs_fn(kc),
                                                start=(i == 0), stop=(i == len(kcs) - 1))
                    return last
                P.add("pe", f, reads=[(rb, None)] + rhs_reads, writes=[(bank, None)])

            def tm_group(rb, kcs, lhs_fn, lhs_reads, bank, start=True, stop=True):
                def f():
                    last = None
                    for i, kc in enumerate(kcs):
                        last = nc.tensor.matmul(bank[:], lhs_fn(kc), rb[:, kc * 512:(kc + 1) * 512],
                                                start=(start and i == 0), stop=(stop and i == len(kcs) - 1))
                    return last
                P.add("pe", f, reads=[(rb, None)] + lhs_reads, writes=[(bank, None)])

            K16 = list(range(16))

            for _ in s0_gen(0):
                pass
            for tau in range(ntiles):
                own = (tau % 2 == 1)
                oj = tau // 2
                UT = uT if own else uT_alt
                def glr_mm(UT=UT):
                    last = None
                    for kc in range(16):
                        last = nc.tensor.matmul(ps[2][0:16, :], wglr[:, kc * 16:(kc + 1) * 16], UT[:, kc, :],
                                                start=(kc == 0), stop=(kc == 15))
                    return last
                P.add("pe", glr_mm, reads=[(wglr, None), (UT, None)], writes=[(ps[2], None)])
                P.add("act", lambda: nc.scalar.copy(out=glrT1[0:16, :], in_=ps[2][0:16, :]), reads=[(ps[2], None)], writes=[(glrT1, None)])
                def gv_half(half, UT=UT):
                    rb = panel(w_in_p[2 + half], 8192)
                    for blk in range(4):
                        bank = dbank()
                        tm_group(rb, K16, lambda kc, blk=blk, UT=UT: UT[:, kc, blk * 128:(blk + 1) * 128], [(UT, None)], bank)
                        evac(lambda blk=blk, half=half: gv_tok[:, blk, half * 512:(half + 1) * 512], lambda bank=bank: bank[:],
                             [(bank, None)], [(gv_tok, (blk, half))])
                rot["nb"] = 2
                gv_half(0)
                for blk in range(4):
                    bank = ps[3 if blk % 2 == 0 else 2]
                    P.add("pe", lambda blk=blk, bank=bank: nc.tensor.matmul(bank[:], glrT1[0:32, blk * 128:(blk + 1) * 128], wg1sb[0:32, :], start=True, stop=True),
                          reads=[(glrT1, None), (wg1sb, None)], writes=[(bank, None)])
                    P.add("act", lambda blk=blk, bank=bank: nc.scalar.activation(out=sp_tok[:, blk, :], in_=bank[:], func=AF.Exp, scale=-1.0),
                          reads=[(bank, None)], writes=[(sp_tok, blk)])
                    P.add("act", lambda blk=blk: nc.scalar.activation(out=sp_tok[:, blk, :], in_=sp_tok[:, blk, :], func=AF.Ln, bias=1.0),
                          reads=[(sp_tok, blk)], writes=[(sp_tok, blk)])
                gv_half(1)
                rot["nb"] = 4
                for h in range(4):
                    bank = ps[4 + h]

                    def bt_mm(h=h, bank=bank):
                        last = None
                        for blk in range(4):
                            last = nc.tensor.matmul(bank[:, blk * 128:(blk + 1) * 128], sp_tok[:, blk, h * 128:(h + 1) * 128], triN(), start=True, stop=True)
                        return last
                    P.add("pe", bt_mm, reads=[(sp_tok, None), (cst, None)], writes=[(bank, None)])
                    P.add("act", lambda h=h, bank=bank: nc.scalar.activation(out=eb[:, h, :], in_=bank[:], func=AF.Exp),
                          reads=[(bank, None)], writes=[(eb, h)])
                    P.add("act", lambda h=h, bank=bank: nc.scalar.activation(out=enb[:, h, :], in_=bank[:], func=AF.Exp, scale=-1.0),
                          reads=[(bank, None)], writes=[(enb, h)])
                    P.add("dve", lambda h=h: nc.vector.tensor_copy(out=decs[:, 1:5, h], in_=eb[:, h, 127:512:128]),
                          reads=[(eb, h)], writes=[(decs, None)])
                if tau == 1:
                    dump("eb1", eb, lambda: eb[:, 0, :], [128, 512])
                rb = panel(w_in_p[1], 8192)
                for h in range(4):
                    bank = dbank()
                    fm_group(rb, h * 128, K16, lambda kc, UT=UT: UT[:, kc, :], [(UT, None)], bank)
                    P.add("dve", lambda h=h, bank=bank: nc.vector.tensor_tensor(out=kiT[:, h, :], in0=bank[:], in1=enb[:, h, :], op=ALU.mult),
                          reads=[(bank, None), (enb, h)], writes=[(kiT, h)])
                if own:
                    rb = panel(w_in_p[0], 8192)
                    for h in range(4):
                        bank = dbank()
                        fm_group(rb, h * 128, K16, lambda kc, UT=UT: UT[:, kc, :], [(UT, None)], bank)
                        P.add("dve", lambda h=h, bank=bank: nc.vector.scalar_tensor_tensor(
                            out=qdT[:, h, :], in0=bank[:], scalar=float(128 ** -0.5), in1=eb[:, h, :], op0=ALU.mult, op1=ALU.mult),
                            reads=[(bank, None), (eb, h)], writes=[(qdT, h)])
                if own:
                    for half in range(2):
                        rb = panel(w_in_p[4 + half], 8192)
                        for cc in range(4):
                            c8 = half * 4 + cc
                            bank = dbank()
                            fm_group(rb, cc * 128, K16, lambda kc, UT=UT: UT[:, kc, :], [(UT, None)], bank)
                            P.add("act", lambda c8=c8, bank=bank: nc.scalar.activation(out=silu_g[:, c8, :], in_=bank[:], func=AF.Silu),
                                  reads=[(bank, None)], writes=[(silu_g, c8)])

                def dense_gen(tau=tau, own=own):
                    for half in range(2):
                        rb = panel(w_in_p[8 + half], 8192)
                        for hh in range(4):
                            h = half * 4 + hh
                            bank = dbank()
                            fm_group(rb, hh * 128, K16, lambda kc, UT=UT: UT[:, kc, :], [(UT, None)], bank)
                            evac(lambda h=h: kst[:, h, :], lambda bank=bank: bank[:], [(bank, None)], [(kst, h)])
                            yield
                    P.add("sp", lambda tau=tau: nc.sync.dma_start(out=kcache[:, :, tau * 512:(tau + 1) * 512].rearrange("h d t -> d h t"), in_=kst[:, :, :]),
                          reads=[(kst, None)], writes=[(kcb, tau)], dma=True)
                    for half in range(2):
                        rb = panel(w_in_p[10 + half], 8192)
                        for blk in range(4):
                            bank = dbank()
                            tm_group(rb, K16, lambda kc, blk=blk, UT=UT: UT[:, kc, blk * 128:(blk + 1) * 128], [(UT, None)], bank)
                            evac(lambda blk=blk, half=half: vst[:, blk, half * 512:(half + 1) * 512], lambda bank=bank: bank[:],
                                 [(bank, None)], [(vst, (blk, half))])
                            yield
                    P.add("sp", lambda tau=tau: nc.sync.dma_start(out=vcache[tau * 4:(tau + 1) * 4].rearrange("b t c -> t b c"), in_=vst[:, :, :]),
                          reads=[(vst, None)], writes=[(vcb, tau)], dma=True)
                    if own:
                        for half in range(2):
                            rb = panel(w_in_p[6 + half], 8192)
                            for hh in range(4):
                                h = half * 4 + hh
                                bank = dbank()
                                fm_group(rb, hh * 128, K16, lambda kc, UT=UT: UT[:, kc, :], [(UT, None)], bank)
                                evac(lambda h=h: qT[:, h, :], lambda bank=bank: bank[:], [(bank, None)], [(qT, h)])
                                yield

                def gla_gen(tau=tau, own=own):
                    for blk in range(4):
                        tb = ps[4]

                        def ktr(blk=blk, tb=tb):
                            last = None
                            for h in range(4):
                                last = nc.tensor.transpose(out=tb[:].bitcast(BF16)[:, h * 128:(h + 1) * 128],
                                                           in_=kiT[:, h, blk * 128:(blk + 1) * 128], identity=idb[:])
                            return last
                        P.add("pe", ktr, reads=[(kiT, None), (idb, None)], writes=[(tb, None)])
                        P.add("dve", lambda blk=blk, tb=tb: nc.vector.tensor_copy(out=k_tok[:, blk, :], in_=tb[:].bitcast(BF16)[:, 0:512]),
                              reads=[(tb, None)], writes=[(k_tok, blk)])
                        yield
                        if own:
                            for h in range(4):
                                P.add("dve", lambda h=h, blk=blk: nc.vector.tensor_scalar(out=S_bf[:, h, :], in0=Tst[:, h, :], scalar1=decs[:, blk, h:h + 1], scalar2=None, op0=ALU.mult),
                                      reads=[(Tst, h), (decs, None)], writes=[(S_bf, h)])
                            scb = ps[4]

                            def sc_mm(blk=blk, scb=scb):
                                last = None
                                for h in range(4):
                                    last = nc.tensor.matmul(scb[:, h * 128:(h + 1) * 128], kiT[:, h, blk * 128:(blk + 1) * 128],
                                                            qdT[:, h, blk * 128:(blk + 1) * 128], start=True, stop=True)
                                return last
                            P.add("pe", sc_mm, reads=[(kiT, None), (qdT, None)], writes=[(scb, None)])
                            P.add("dve", lambda scb=scb: nc.vector.tensor_tensor(
                                out=scm[:, :, :], in0=scb[:].rearrange("p (h c) -> p h c", h=4),
                                in1=maskC().unsqueeze(1).to_broadcast([128, 4, 128]), op=ALU.mult),
                                reads=[(scb, None), (cst, None)], writes=[(scm, None)])
                            yield
                            for b2 in range(2):
                                ob = ps[5 + b2]

                                def o_mm(b2=b2, ob=ob, blk=blk):
                                    last = None
                                    for i in range(4):
                                        idx = b2 * 4 + i
                                        h, ec = idx // 2, idx % 2
                                        o = ob[:, i * 128:(i + 1) * 128]
                                        nc.tensor.matmul(o, S_bf[:, h, ec * 128:(ec + 1) * 128], qdT[:, h, blk * 128:(blk + 1) * 128], start=True, stop=False)
                                        last = nc.tensor.matmul(o, gv_tok[:, blk, h * 256 + ec * 128: h * 256 + (ec + 1) * 128], scm[:, h, :], start=False, stop=True)
                                    return last
                                P.add("pe", o_mm, reads=[(S_bf, None), (qdT, None), (gv_tok, None), (scm, None)], writes=[(ob, None)])
                                P.add("act", lambda b2=b2, ob=ob: nc.scalar.activation(out=sq[:, b2 * 4:(b2 + 1) * 4, :], in_=ob[:].rearrange("p (a c) -> p a c", a=4), func=AF.Square),
                                      reads=[(ob, None)], writes=[(sq, b2)])
                                yield
                            ssb = ps[4]

                            def ss_mm(ssb=ssb):
                                last = None
                                for h in range(4):
                                    nc.tensor.matmul(ssb[:, h * 128:(h + 1) * 128], onesb[:], sq[:, 2 * h, :], start=True, stop=False)
                                    last = nc.tensor.matmul(ssb[:, h * 128:(h + 1) * 128], onesb[:], sq[:, 2 * h + 1, :], start=False, stop=True)
                                return last
                            P.add("pe", ss_mm, reads=[(onesb, None), (sq, None)], writes=[(ssb, None)])
                            P.add("act", lambda ssb=ssb: nc.scalar.activation(out=rstdg[:, :, :], in_=ssb[:].rearrange("p (h c) -> p h c", h=4), func=AF.Ln, scale=1.0 / 256, bias=EPS),
                                  reads=[(ssb, None)], writes=[(rstdg, None)])
                            P.add("act", lambda: nc.scalar.activation(out=rstdg[:, :, :], in_=rstdg[:, :, :], func=AF.Exp, scale=-0.5),
                                  reads=[(rstdg, None)], writes=[(rstdg, None)])
                            yield
                            for b2 in range(2):
                                ob = ps[5 + b2]
                                for ec in range(2):
                                    P.add("dve", lambda b2=b2, ob=ob, ec=ec: nc.vector.scalar_tensor_tensor(
                                        out=tmp_o[:, :, :].rearrange("p (h e) c -> p h e c", e=2)[:, b2 * 2:(b2 + 1) * 2, ec, :],
                                        in0=ob[:].rearrange("p (h e c) -> p h e c", h=2, e=2)[:, :, ec, :],
                                        scalar=gfm[:, 48 + ec:49 + ec], in1=rstdg[:, b2 * 2:(b2 + 1) * 2, :], op0=ALU.mult, op1=ALU.mult),
                                        reads=[(ob, None), (gfm, None), (rstdg, None)], writes=[(tmp_o, (b2, ec))])
                            P.add("dve", lambda blk=blk: nc.vector.tensor_tensor(out=o_aT[:, :, blk * 128:(blk + 1) * 128], in0=tmp_o[:, :, :],
                                                                                 in1=silu_g[:, :, blk * 128:(blk + 1) * 128], op=ALU.mult),
                                  reads=[(tmp_o, None), (silu_g, None)], writes=[(o_aT, blk)])
                            yield
                        for b2 in range(2):
                            kb_ = ps[7]

                            def kv_mm(b2=b2, kb_=kb_, blk=blk):
                                last = None
                                for i in range(2):
                                    h = b2 * 2 + i
                                    last = nc.tensor.matmul(kb_[:, i * 256:(i + 1) * 256], k_tok[:, blk, h * 128:(h + 1) * 128],
                                                            gv_tok[:, blk, h * 256:(h + 1) * 256], start=True, stop=True)
                                return last
                            P.add("pe", kv_mm, reads=[(k_tok, blk), (gv_tok, None)], writes=[(kb_, None)])
                            for i in range(2):
                                h = b2 * 2 + i
                                P.add("dve", lambda h=h, i=i, kb_=kb_, blk=blk: nc.vector.scalar_tensor_tensor(
                                    out=Tst[:, h, :], in0=Tst[:, h, :], scalar=decs[:, blk, h:h + 1], in1=kb_[:, i * 256:(i + 1) * 256],
                                    op0=ALU.mult, op1=ALU.add), reads=[(Tst, h), (decs, None), (kb_, None)], writes=[(Tst, h)])
                            yield

                gens = [gla_gen(), dense_gen()]
                if (not own) and tau + 1 < ntiles:
                    gens.append(s0_gen(tau + 1))
                rot["nb"] = 2
                interleave(*gens)
                rot["nb"] = 4
                P.add("dve", lambda: nc.vector.tensor_copy(out=decs[:, 0, :], in_=decs[:, 4, :]), reads=[(decs, None)], writes=[(decs, None)])
                if tau == 1:
                    dump("o_aT1", o_aT, lambda: o_aT[:, 0, :], [128, 512])
                if not own:
                    continue
                if stop_after == "S2":
                    break
                scale = float(128 ** -0.5)
                steps = [(kap, kb) for kap in range(tau, -1, -1) for kb in (3, 2, 1, 0)]
                gate_thunks = []
                gstate = {"b": 0}
                for (pidx0, gs) in ((12, gsa), (16, gsb)):
                    for j in range(4):
                        holder = {}
                        for fc in range(4):
                            f = 4 * j + fc

                            def tA(holder=holder, pidx=pidx0 + j, fc=fc):
                                if fc == 0:
                                    holder["rb"] = panel(w_in_p[pidx], 8192)
                                rb = holder["rb"]
                                gstate["b"] ^= 1
                                bank = ps[2 + gstate["b"]]
                                holder["bank"] = bank

                                def mmA(rb=rb, bank=bank, fc=fc):
                                    last = None
                                    for kc in range(8):
                                        last = nc.tensor.matmul(bank[:], rb[:, kc * 512 + fc * 128: kc * 512 + fc * 128 + 128], uT[:, kc, :], start=(kc == 0), stop=False)
                                    return last
                                P.add("pe", mmA, reads=[(rb, None), (uT, None)], writes=[(bank, None)])

                            def tB(holder=holder, fc=fc, f=f, gs=gs):
                                rb = holder["rb"]
                                bank = holder["bank"]

                                def mmB(rb=rb, bank=bank, fc=fc):
                                    last = None
                                    for kc in range(8, 16):
                                        last = nc.tensor.matmul(bank[:], rb[:, kc * 512 + fc * 128: kc * 512 + fc * 128 + 128], uT[:, kc, :], start=False, stop=(kc == 15))
                                    return last
                                P.add("pe", mmB, reads=[(rb, None), (uT, None)], writes=[(bank, None)])
                                tmp = gtmp[f % 2]
                                P.add("act", lambda bank=bank, tmp=tmp: nc.scalar.activation(out=tmp[:], in_=bank[:], func=AF.Exp, scale=-1.0),
                                      reads=[(bank, None)], writes=[(tmp, None)])
                                P.add("dve", lambda tmp=tmp: nc.vector.tensor_scalar_add(out=tmp[:], in0=tmp[:], scalar1=1.0),
                                      reads=[(tmp, None)], writes=[(tmp, None)])
                                def rcp(tmp=tmp, gs=gs, f=f):
                                    with nc.allow_low_precision("gate sigmoid stored as bf16 (matmul-operand precision)"):
                                        return nc.vector.reciprocal(out=gs[:, f, :], in_=tmp[:])
                                P.add("dve", rcp, reads=[(tmp, None)], writes=[(gs, f)])
                            gate_thunks.append(tA)
                            gate_thunks.append(tB)
                for hp in range(8 // NH):
                    hs = tuple(NH * hp + i for i in range(NH))
                    psZ = tuple(ps[i % 2] for i in range(NH))
                    psC = tuple(ps[i % 2] for i in range(NH))
                    psO = tuple(ps[4 + i] for i in range(NH))
                    for hh in range(NH):
                        h = hs[hh]
                        P.add("dve", lambda hh=hh: nc.vector.memset(SPs[hh][:], 0.0), writes=[(SPs[hh], None)])
                        P.add("pe", lambda hh=hh, h=h: nc.tensor.matmul(psO[hh][:], zerosb[:], qT[:, h, :], start=True, stop=False),
                              reads=[(zerosb, None), (qT, h)], writes=[(psO[hh], None)])
                    def kv_load(kap):
                        sl = kap % 2
                        for hh in range(NH):
                            h = hs[hh]
                            P.add("sp", lambda hh=hh, h=h, kap=kap, sl=sl: nc.sync.dma_start(out=Kt[hh][sl][:], in_=kcache[h, :, kap * 512:(kap + 1) * 512]),
                                  reads=[(kcb, kap)], writes=[(Kt[hh][sl], None)], dma=True)
                            P.add("sp", lambda hh=hh, h=h, kap=kap, sl=sl: nc.sync.dma_start(
                                out=Vt[hh][sl][:, :, :], in_=vcache[kap * 4:(kap + 1) * 4, :, h * 128:(h + 1) * 128].rearrange("b t d -> t b d")),
                                reads=[(vcb, kap)], writes=[(Vt[hh][sl], None)], dma=True)

                    def stage_qk(si):
                        kap, kb = steps[si]
                        diag = kap == tau
                        qlo = kb * 128 if diag else 0
                        sl = kap % 2
                        for hh in range(NH):
                            h = hs[hh]
                            P.add("pe", lambda hh=hh, h=h, sl=sl, kb=kb, qlo=qlo: nc.tensor.matmul(
                                psZ[hh][:, qlo:512], Kt[hh][sl][:, kb * 128:(kb + 1) * 128], qT[:, h, qlo:512], start=True, stop=True),
                                reads=[(Kt[hh][sl], None), (qT, h)], writes=[(psZ[hh], None)])
                            P.add("act", lambda hh=hh, qlo=qlo: nc.scalar.activation(out=Eb[hh][:, qlo:512], in_=psZ[hh][:, qlo:512], func=AF.Exp, scale=scale),
                                  reads=[(psZ[hh], None)], writes=[(Eb[hh], None)])
                            if diag:
                                P.add("dve", lambda hh=hh, qlo=qlo: nc.vector.tensor_tensor(out=Eb[hh][:, qlo:qlo + 128], in0=Eb[hh][:, qlo:qlo + 128], in1=maskS(), op=ALU.mult),
                                      reads=[(Eb[hh], None), (cst, None)], writes=[(Eb[hh], None)])

                    kv_load(tau)
                    if tau >= 1:
                        kv_load(tau - 1)
                    stage_qk(0)
                    for si, (kap, kb) in enumerate(steps):
                        first = si == 0
                        last_ = si == len(steps) - 1
                        diag = kap == tau
                        qlo = kb * 128 if diag else 0
                        sl = kap % 2
                        if kb == 3 and kap < tau and kap >= 1:
                            kv_load(kap - 1)
                        for hh in range(NH):
                            P.add("act", lambda hh=hh, qlo=qlo: nc.scalar.activation(out=SPb[hh][:, qlo:512], in_=Eb[hh][:, qlo:512], func=AF.Ln, bias=1.0),
                                  reads=[(Eb[hh], None)], writes=[(SPb[hh], None)])
                        for pair in range(NH // 2):
                            for hh in (2 * pair, 2 * pair + 1):
                                def c_mm(hh=hh, qlo=qlo, first=first):
                                    i1 = nc.tensor.matmul(psC[hh][:, qlo:512], negUb[:], SPb[hh][:, qlo:512], start=True, stop=first)
                                    if not first:
                                        i1 = nc.tensor.matmul(psC[hh][:, qlo:512], negonesb[:], SPs[hh][:, qlo:512], start=False, stop=True)
                                    return i1
                                P.add("pe", c_mm, reads=[(negUb, None), (SPb[hh], None), (SPs[hh], None), (negonesb, None)], writes=[(psC[hh], None)])
                            for hh in (2 * pair, 2 * pair + 1):
                                P.add("act", lambda hh=hh, qlo=qlo: nc.scalar.activation(out=ECb[hh][:, qlo:512], in_=psC[hh][:, qlo:512], func=AF.Exp),
                                      reads=[(psC[hh], None)], writes=[(ECb[hh], None)])
                        for hh in range(NH):
                            P.add("dve", lambda hh=hh, qlo=qlo: nc.vector.tensor_tensor(out=Ab[hh][:, qlo:512], in0=Eb[hh][:, qlo:512], in1=ECb[hh][:, qlo:512], op=ALU.mult),
                                  reads=[(Eb[hh], None), (ECb[hh], None)], writes=[(Ab[hh], None)])
                            if not last_:
                                P.add("dve", lambda hh=hh, qlo=qlo: nc.vector.tensor_tensor(out=SPs[hh][:, qlo:512], in0=SPs[hh][:, qlo:512], in1=SPb[hh][:, qlo:512], op=ALU.add),
                                      reads=[(SPs[hh], None), (SPb[hh], None)], writes=[(SPs[hh], None)])
                        if not last_:
                            stage_qk(si + 1)
                        if gate_thunks:
                            gate_thunks.pop(0)()
                        for hh in range(NH):
                            P.add("pe", lambda hh=hh, sl=sl, kb=kb, qlo=qlo, last_=last_: nc.tensor.matmul(
                                psO[hh][:, qlo:512], Vt[hh][sl][:, kb, :], Ab[hh][:, qlo:512], start=False, stop=last_),
                                reads=[(Vt[hh][sl], None), (Ab[hh], None)], writes=[(psO[hh], None)])
                    for hh in range(NH):
                        h = hs[hh]
                        P.add("act", lambda hh=hh, h=h: nc.scalar.copy(out=o_bT[:, h, :], in_=psO[hh][:]), reads=[(psO[hh], None)], writes=[(o_bT, h)])
                if tau == 1:
                    dump("o_bT1", o_bT, lambda: o_bT[:, 0, :], [128, 512])
                if stop_after == "S3":
                    break
                while gate_thunks:
                    gate_thunks.pop(0)()
                K8 = list(range(8))
                for j in range(4):
                    rb = panel(w_bg[j], 4096)
                    for fc in range(4):
                        bank = dbank()
                        fm_group(rb, fc * 128, K8, lambda kc: o_aT[:, kc, :], [(o_aT, None)], bank)
                        P.add("dve", lambda fc=fc, j=j, bank=bank: nc.vector.tensor_tensor(out=sga[fc][:], in0=bank[:], in1=gsa[:, 4 * j + fc, :], op=ALU.mult),
                              reads=[(bank, None), (gsa, 4 * j + fc)], writes=[(sga[fc], None)])
                    rb = panel(w_bs[j], 4096)
                    for fc in range(4):
                        bank = dbank()
                        fm_group(rb, fc * 128, K8, lambda kc: o_bT[:, kc, :], [(o_bT, None)], bank)
                        P.add("dve", lambda fc=fc, j=j, bank=bank: nc.vector.tensor_tensor(out=sgb[fc][:], in0=bank[:], in1=gsb[:, 4 * j + fc, :], op=ALU.mult),
                              reads=[(bank, None), (gsb, 4 * j + fc)], writes=[(sgb[fc], None)])
                        P.add("pool", lambda fc=fc, j=j: nc.gpsimd.tensor_tensor(out=yT[:, 4 * j + fc, :], in0=sga[fc][:], in1=sgb[fc][:], op=ALU.add),
                              reads=[(sga[fc], None), (sgb[fc], None)], writes=[(yT, 4 * j + fc)])
                for blk in range(4):
                    P.add("sp", lambda tau=tau, blk=blk: nc.sync.dma_start(out=xh[blk][:], in_=xs[tau, blk]), writes=[(xh[blk], None)], dma=True)
                P.add("sp", lambda: nc.sync.dma_start(out=gbc[:], in_=gtm_d[0]), writes=[(gbc, None)], dma=True)
                if stop_after == "S4a":
                    break
                if tau == 1:
                    dump("yT1", yT, lambda: yT[:, 0, :], [128, 512])
                for j in range(4):
                    rb = panel(w_out_p[j], 8192)
                    for blk in range(4):
                        bank = dbank()
                        tm_group(rb, K16, lambda kc, blk=blk: yT[:, kc, blk * 128:(blk + 1) * 128], [(yT, None)], bank)
                        P.add("dve", lambda blk=blk, j=j, bank=bank: nc.vector.tensor_copy(out=m_sb[blk][:, j * 512:(j + 1) * 512], in_=bank[:]),
                              reads=[(bank, None)], writes=[(m_sb[blk], j)])

                def post_norm_add(srcs, gb, gkey):
                    for blk in range(4):
                        c0 = 8 + 3 * (blk % 2)
                        jn = xn[blk % 2]
                        P.add("act", lambda blk=blk, jn=jn, c0=c0: nc.scalar.activation(out=jn[:], in_=srcs[blk][:], func=AF.Square, accum_out=sst[:, c0:c0 + 1]),
                              reads=[(srcs[blk], None)], writes=[(jn, None), (sst, c0)])
                        P.add("act", lambda c0=c0: nc.scalar.activation(out=sst[:, c0 + 1:c0 + 2], in_=sst[:, c0:c0 + 1], func=AF.Ln, scale=1.0 / 2048, bias=EPS),
                              reads=[(sst, c0)], writes=[(sst, c0 + 1)])
                        P.add("act", lambda c0=c0: nc.scalar.activation(out=sst[:, c0 + 2:c0 + 3], in_=sst[:, c0 + 1:c0 + 2], func=AF.Exp, scale=-0.5),
                              reads=[(sst, c0 + 1)], writes=[(sst, c0 + 2)])
                        P.add("dve", lambda blk=blk, c0=c0: nc.vector.scalar_tensor_tensor(out=srcs[blk][:], in0=srcs[blk][:], scalar=sst[:, c0 + 2:c0 + 3], in1=gb[:],
                                                                                  op0=ALU.mult, op1=ALU.mult),
                              reads=[(srcs[blk], None), (sst, c0 + 2), (gb, None)], writes=[(srcs[blk], None)])
                        P.add("pool", lambda blk=blk: nc.gpsimd.tensor_tensor(out=xh[blk][:], in0=xh[blk][:], in1=srcs[blk][:], op=ALU.add),
                              reads=[(xh[blk], None), (srcs[blk], None)], writes=[(xh[blk], None)])
                post_norm_add(m_sb, gbc, 0)
                if tau == 1:
                    dump("h1", xh[0], lambda: xh[0][:, 0:512], [128, 512])
                if stop_after == "S4":
                    break
                for blk in range(4):
                    norm_transpose(xh[blk], lambda blk=blk: xh[blk][:], 16, blk)
                for half in range(2):
                    for pj in range(8):
                        rb = panel(w_up_p[half * 8 + pj], 8192)
                        for fc in range(4):
                            ci = pj * 4 + fc
                            bank = dbank()
                            fm_group(rb, fc * 128, K16, lambda kc: uT[:, kc, :], [(uT, None)], bank)
                            rt = rtmp[ci % 2]
                            P.add("act", lambda bank=bank, rt=rt: nc.scalar.activation(out=rt[:], in_=bank[:], func=AF.Relu),
                                  reads=[(bank, None)], writes=[(rt, None)])
                            P.add("dve", lambda ci=ci, rt=rt: nc.vector.tensor_tensor(out=aT[:, ci, :], in0=rt[:], in1=rt[:], op=ALU.mult),
                                  reads=[(rt, None)], writes=[(aT, ci)])
                    for j in range(4):
                        for s2 in range(2):
                            rb = panel(w_dn_p[j, half * 2 + s2], 8192)
                            for blk in range(4):
                                bank = ps[4 + blk]
                                tm_group(rb, K16, lambda kc, blk=blk, s2=s2: aT[:, s2 * 16 + kc, blk * 128:(blk + 1) * 128], [(aT, None)], bank,
                                         start=(s2 == 0), stop=(s2 == 1))
                        for blk in range(4):
                            bank = ps[4 + blk]
                            if half == 0:
                                P.add("dve", lambda blk=blk, j=j, bank=bank: nc.vector.tensor_copy(out=f_sb[blk][:, j * 512:(j + 1) * 512], in_=bank[:]),
                                      reads=[(bank, None)], writes=[(f_sb[blk], j)])
                            else:
                                P.add("dve", lambda blk=blk, j=j, bank=bank: nc.vector.tensor_tensor(out=f_sb[blk][:, j * 512:(j + 1) * 512], in0=f_sb[blk][:, j * 512:(j + 1) * 512], in1=bank[:], op=ALU.add),
                                      reads=[(bank, None), (f_sb[blk], j)], writes=[(f_sb[blk], j)])
                P.add("sp", lambda: nc.sync.dma_start(out=gbc2[:], in_=gtm_d[1]), writes=[(gbc2, None)], dma=True)
                post_norm_add(f_sb, gbc2, 1)
                if tau == 1:
                    dump("h2", xh[0], lambda: xh[0][:, 0:512], [128, 512])
                if stop_after == "S5":
                    break
                for blk in range(4):
                    norm_transpose(xh[blk], lambda blk=blk: xh[blk][:], 32, blk)
                P.add("sp", lambda oj=oj: nc.sync.dma_start(out=p_f[:, :, :], in_=pp[oj].rearrange("b t c -> t b c")), writes=[(p_f, None)], dma=True)
                P.add("dve", lambda: nc.vector.tensor_copy(out=p_b[:, :, :], in_=p_f[:, :, :]), reads=[(p_f, None)], writes=[(p_b, None)])
                for blk in range(4):
                    tb = ps[0]

                    def ptr(blk=blk, tb=tb):
                        last = None
                        for kc in range(2):
                            last = nc.tensor.transpose(out=tb[:].bitcast(BF16)[:, kc * 128:(kc + 1) * 128], in_=p_b[:, blk, kc * 128:(kc + 1) * 128], identity=idb[:])
                        return last
                    P.add("pe", ptr, reads=[(p_b, None), (idb, None)], writes=[(tb, None)])
                    P.add("dve", lambda blk=blk, tb=tb: nc.vector.tensor_copy(out=pT[:, :, blk * 128:(blk + 1) * 128], in_=tb[:].bitcast(BF16)[:, 0:256].rearrange("p (a b) -> p a b", a=2)),
                          reads=[(tb, None)], writes=[(pT, blk)])
                def s6_gen(oj=oj):
                    for j in range(4):
                        rbg = panel(w_pg_p[j], 8192)
                        for blk in range(4):
                            gbk = ps[blk % 2]
                            tm_group(rbg, K16, lambda kc, blk=blk: uT[:, kc, blk * 128:(blk + 1) * 128], [(uT, None)], gbk)
                            P.add("act", lambda gbk=gbk, blk=blk: nc.scalar.activation(out=sgp[blk][:], in_=gbk[:], func=AF.Sigmoid), reads=[(gbk, None)], writes=[(sgp[blk], None)])
                            yield
                        rbp = panel(w_pp_p[j], 1024)
                        for blk in range(4):
                            ebk = ps[4 + blk % 2]
                            tm_group(rbp, [0, 1], lambda kc, blk=blk: pT[:, kc, blk * 128:(blk + 1) * 128], [(pT, None)], ebk)
                            P.add("dve", lambda ebk=ebk, blk=blk: nc.vector.tensor_tensor(out=sgp[blk][:], in0=ebk[:], in1=sgp[blk][:], op=ALU.mult),
                                  reads=[(ebk, None), (sgp[blk], None)], writes=[(sgp[blk], None)])
                            P.add("pool", lambda blk=blk, j=j: nc.gpsimd.tensor_tensor(out=xh[blk][:, j * 512:(j + 1) * 512], in0=xh[blk][:, j * 512:(j + 1) * 512], in1=sgp[blk][:], op=ALU.add),
                                  reads=[(xh[blk], None), (sgp[blk], None)], writes=[(xh[blk], None)])
                            yield
                    for blk in range(4):
                        P.add("sp", lambda oj=oj, blk=blk: nc.sync.dma_start(out=y[oj, blk], in_=xh[blk][:]), reads=[(xh[blk], None)], dma=True)

                gens = [s6_gen()]
                if tau + 1 < ntiles:
                    gens.append(s0_gen(tau + 1))
                interleave(*gens)

        sstm = sb("sstm", [128, 16], F32)
        P0 = Prog(nc, spaces, plan=True)
        run(P0)
        P1 = Prog(nc, spaces, plan=False)
        P1.panel_list = P0.panel_log
        run(P1)
        nw = P1.emit(es)
        DEBUG["nwaits"] = nw
        DEBUG["nops"] = P1.nops
        DEBUG["npanels"] = len(P0.panel_log)
    return nc, dbg_out


def _panelize(W):
    K, N = W.shape
    return np.ascontiguousarray(W.reshape(K // 128, 128, N // 512, 512).transpose(2, 1, 0, 3).reshape(N // 512, 128, (K // 128) * 512))


def prep_shared(inp):
    f = lambda a: np.asarray(a, dtype=np.float32)
    w_in = f(inp["w_in"])[0]
    cols = {"gq": (0, 512), "gk": (512, 512), "gv": (1024, 1024), "glr": (2048, 16), "gout": (2064, 1024), "sq": (3088, 1024),
            "sk": (4112, 1024), "sv": (5136, 1024), "ga": (6160, 2048), "gb": (8208, 2048)}
    order = ["gq", "gk", "gv", "gout", "sq", "sk", "sv", "ga", "gb"]
    w_in_p = np.concatenate([_panelize(w_in[:, cols[k][0]:cols[k][0] + cols[k][1]]) for k in order], 0)
    assert w_in_p.shape == (20, 128, 8192)
    glr = w_in[:, 2048:2064]
    w_glr = np.ascontiguousarray(glr.reshape(16, 128, 16).transpose(1, 0, 2).reshape(128, 256))
    wg1 = np.zeros((32, 512), np.float32)
    wg1[0:16] = f(inp["w_gate_up"])[0]
    wg1[16] = f(inp["b_gate"])[0]
    w_dn = f(inp["w_mlp_down"])[0]
    w_dn_p = np.ascontiguousarray(w_dn.reshape(4, 16, 128, 4, 512).transpose(3, 0, 2, 1, 4).reshape(4, 4, 128, 8192))
    gfm = np.zeros((128, 64), np.float32)
    gfm[:, 0:16] = f(inp["norm_mix_pre"])[0].reshape(16, 128).T
    gfm[:, 16:32] = f(inp["norm_mlp_pre"])[0].reshape(16, 128).T
    gfm[:, 32:48] = f(inp["norm_ple"])[0].reshape(16, 128).T
    gfm[:, 48:50] = f(inp["gla_norm"])[0].reshape(2, 128).T
    gtm = np.stack([np.tile(f(inp["norm_mix_post"])[0][None, :], (128, 1)), np.tile(f(inp["norm_mlp_post"])[0][None, :], (128, 1))], 0)
    i = np.arange(128)
    consts = np.zeros((128, 640), np.float32)
    consts[:, 0:128] = np.eye(128)
    consts[:, 128:256] = (i[:, None] < i[None, :])
    consts[:, 256:384] = -1.0 * (i[:, None] >= i[None, :])
    consts[:, 384:512] = (-1.0 / 16.0) * (i[:, None] <= i[None, :])
    consts[:, 512:640] = (i[:, None] <= i[None, :])
    return {
        "w_in_p": w_in_p, "w_glr": w_glr, "wg1": wg1,
        "w_bg": _panelize(f(inp["w_branch_gla"])[0]), "w_bs": _panelize(f(inp["w_branch_sb"])[0]),
        "w_out_p": _panelize(f(inp["w_out"])[0]), "w_up_p": _panelize(f(inp["w_mlp_up"])[0]), "w_dn_p": w_dn_p,
        "w_pg_p": _panelize(f(inp["w_ple_gate"])[0]), "w_pp_p": _panelize(f(inp["w_ple_proj"])[0]),
        "gfm": gfm, "gtm": np.ascontiguousarray(gtm), "consts": consts,
    }


def prep_core(inp, b, c):
    x = np.asarray(inp["x"], dtype=np.float32)
    p = np.asarray(inp["p"], dtype=np.float32)
    xs = np.zeros((NT, 512, 2048), np.float32)
    pp = np.zeros((4, 512, 256), np.float32)
    for j in range(4):
        if c == 1:
            xs[2 * j] = x[b, (2 * j) * 512:(2 * j + 1) * 512]
        elif j >= 1:
            xs[2 * j] = x[b, (2 * j - 1) * 512:(2 * j) * 512]
        g = 2 * j + c
        xs[2 * j + 1] = x[b, g * 512:(g + 1) * 512]
        pp[j] = p[0, b, g * 512:(g + 1) * 512]
    return {"xs": xs.reshape(NT, 4, 128, 2048), "pp": pp.reshape(4, 4, 128, 256)}


_CACHE = {}


def kernel(**inputs):
    if "nc" not in _CACHE:
        _CACHE["nc"] = build()[0]
    nc = _CACHE["nc"]
    shared = prep_shared(inputs)
    in_maps = []
    for core in range(8):
        b, c = core // 2, core % 2
        m = dict(shared)
        m.update(prep_core(inputs, b, c))
        in_maps.append(m)
    res = run_bass_kernel_spmd(nc, in_maps, core_ids=list(range(8)))
    out = np.zeros((4, 4096, 2048), np.float32)
    for core in range(8):
        b, c = core // 2, core % 2
        yy = np.asarray(res.results[core]["y"]).reshape(4, 512, 2048)
        for j in range(4):
            g = 2 * j + c
            out[b, g * 512:(g + 1) * 512] = yy[j]
    return out
```
